# Optimizing a Trainium2 kernel written in Bass

```python
import math
import jax, jax.numpy as jnp
from jax import lax
import numpy as np

D_MODEL = 2048
BATCH = 4
SEQ = 2048
DEPTH = 4
DEC_BATCH = 8
DEC_SEQ = 8
PAST_LEN = 16384
PAGE_SIZE = 128

N_A = DEPTH // 2
N_B = DEPTH - N_A
RWKV_HEAD = 64
RWKV_HEADS = D_MODEL // RWKV_HEAD
D_DECAY_LORA = max(32, int(round(1.8 * D_MODEL ** 0.5 / 32)) * 32)
D_AAA_LORA = max(32, int(round(1.8 * D_MODEL ** 0.5 / 32)) * 32)
D_MV_LORA = max(32, int(round(1.3 * D_MODEL ** 0.5 / 32)) * 32)
D_GATE_LORA = max(32, int(round(0.6 * D_MODEL ** 0.8 / 32)) * 32)
ATT_HEAD = 128
ATT_HEADS = D_MODEL // (2 * ATT_HEAD)
ATT_WIDTH = ATT_HEADS * 2 * ATT_HEAD
D_FF = 4 * D_MODEL
ROPE_THETA = 10000.0
Q_BLOCK = 128
NORM_EPS = 1e-6
GN_EPS = 64e-5
SUBLN_EPS = 1e-5

kernel_name = "yoco_rwkv7_diff_attn_step"


def rms_norm(x, g, eps=NORM_EPS):
    x32 = x.astype(jnp.float32)
    y = x32 * lax.rsqrt(jnp.mean(x32 * x32, axis=-1, keepdims=True) + eps)
    return (y * g.astype(jnp.float32)).astype(x.dtype)


def sq_relu_mlp(x, w1, w2):
    h = jax.nn.relu(x @ w1)
    return (h * h) @ w2


def rope(x, pos):
    half = x.shape[-1] // 2
    inv = jnp.power(ROPE_THETA, -jnp.arange(half, dtype=jnp.float32) / half)
    ang = pos.astype(jnp.float32)[:, None] * inv[None, :]
    cos = jnp.cos(ang)[None, :, None, None, :]
    sin = jnp.sin(ang)[None, :, None, None, :]
    x32 = x.astype(jnp.float32)
    x1, x2 = x32[..., :half], x32[..., half:]
    return jnp.concatenate([x1 * cos - x2 * sin, x2 * cos + x1 * sin], axis=-1).astype(x.dtype)


def wkv7_scan(r, w, k, v, a, b, s0):
    def step(s, inp):
        r_t, w_t, k_t, v_t, a_t, b_t = inp
        sa = jnp.einsum('bhvk,bhk->bhv', s, a_t)
        s = s * w_t[:, :, None, :] + sa[..., None] * b_t[:, :, None, :] + v_t[..., None] * k_t[:, :, None, :]
        y = jnp.einsum('bhvk,bhk->bhv', s, r_t)
        return s, y
    seq = tuple(jnp.swapaxes(t, 0, 1) for t in (r, w, k, v, a, b))
    s_final, y = lax.scan(step, s0, seq)
    return jnp.swapaxes(y, 0, 1), s_final


def rwkv7_time_mix(xn, shift0, s0, v_first, p, vres):
    mu, vec, wr, wk, wv, wo, w1, w2, a1, a2, g1, g2, rk = p
    Bsz, T, D = xn.shape
    H, N = RWKV_HEADS, RWKV_HEAD
    x_prev = jnp.concatenate([shift0[:, None, :].astype(xn.dtype), xn[:, :-1]], axis=1)
    xx = x_prev - xn
    xr, xw, xk, xv, xa, xg = (xn + xx * mu[i] for i in range(6))
    w0, a0, k_k, k_a, lnx_w, lnx_b = (vec[i] for i in range(6))
    r = xr @ wr
    w_raw = -jax.nn.softplus(-(w0 + jnp.tanh(xw @ w1) @ w2)) - 0.5
    k = xk @ wk
    v = xv @ wv
    if vres is None:
        v_first = v
    else:
        v0, v1, v2 = vres
        v = v + (v_first - v) * jax.nn.sigmoid(v0 + (xv @ v1) @ v2)
    a = jax.nn.sigmoid(a0 + (xa @ a1) @ a2)
    g = jax.nn.sigmoid(xg @ g1) @ g2
    hd = lambda t: t.reshape(Bsz, T, H, N).astype(jnp.float32)
    kk = hd(k * k_k)
    kk = kk / jnp.maximum(jnp.sqrt(jnp.sum(kk * kk, axis=-1, keepdims=True)), 1e-12)
    k = k * (1 + (a - 1) * k_a)
    r_h, k_h, v_h, a_h = hd(r), hd(k), hd(v), hd(a)
    decay = jnp.exp(-jnp.exp(hd(w_raw)))
    y, s_new = wkv7_scan(r_h, decay, k_h, v_h, -kk, kk * a_h, s0.astype(jnp.float32))
    mean = jnp.mean(y, axis=-1, keepdims=True)
    var = jnp.mean(jnp.square(y - mean), axis=-1, keepdims=True)
    y = ((y - mean) * lax.rsqrt(var + GN_EPS)).reshape(Bsz, T, D)
    y = y * lnx_w.astype(jnp.float32) + lnx_b.astype(jnp.float32)
    bonus = jnp.sum(r_h * k_h * rk.astype(jnp.float32), axis=-1, keepdims=True) * v_h
    y = (y + bonus.reshape(Bsz, T, D)).astype(xn.dtype)
    out = (y * g) @ wo
    return out, xn[:, -1], s_new.astype(s0.dtype), v_first


def lambda_init(layer_idx):
    return 0.8 - 0.6 * math.exp(-0.3 * layer_idx)


def diff_attn_prompt(q, k, v, lam):
    Bsz, T = q.shape[:2]
    nb = T // Q_BLOCK
    qb = jnp.swapaxes(q.reshape(Bsz, nb, Q_BLOCK, ATT_HEADS, 2, ATT_HEAD), 0, 1)
    k_pos = jnp.arange(T)
    scale = ATT_HEAD ** -0.5

    def block(args):
        i, q_blk = args
        s = jnp.einsum('bqhmd,bkhmd->bhmqk', q_blk, k).astype(jnp.float32) * scale
        q_pos = i * Q_BLOCK + jnp.arange(Q_BLOCK)
        s = jnp.where(k_pos[None, :] <= q_pos[:, None], s, -jnp.inf)
        p = jax.nn.softmax(s, axis=-1).astype(v.dtype)
        o = jnp.einsum('bhmqk,bkhe->bqhme', p, v)
        return o[..., 0, :] - lam * o[..., 1, :]

    out = lax.map(block, (jnp.arange(nb), qb))
    return jnp.swapaxes(out, 0, 1).reshape(Bsz, T, ATT_HEADS, 2 * ATT_HEAD)


def diff_attn_sample(q, k_new, v_new, k_past, v_past, lam):
    Tq = q.shape[1]
    P = k_past.shape[1]
    scale = ATT_HEAD ** -0.5
    s_past = jnp.einsum('bqhmd,bkhmd->bhmqk', q, k_past).astype(jnp.float32) * scale
    s_new = jnp.einsum('bqhmd,bkhmd->bhmqk', q, k_new).astype(jnp.float32) * scale
    causal = jnp.arange(Tq)[None, :] <= jnp.arange(Tq)[:, None]
    s_new = jnp.where(causal, s_new, -jnp.inf)
    p = jax.nn.softmax(jnp.concatenate([s_past, s_new], axis=-1), axis=-1).astype(v_new.dtype)
    o = (jnp.einsum('bhmqk,bkhe->bqhme', p[..., :P], v_past)
         + jnp.einsum('bhmqk,bkhe->bqhme', p[..., P:], v_new))
    return o[..., 0, :] - lam * o[..., 1, :]


def setup_inputs(seed: int = 0) -> dict:
    key = jax.random.key(seed)
    keys = list(jax.random.split(key, 64))

    def nk():
        return keys.pop()

    f32 = jnp.float32

    def normal(shape, scale):
        return jax.random.normal(nk(), shape, f32) * scale

    def gain(shape):
        return 1.0 + normal(shape, 0.05)

    D, H, N = D_MODEL, RWKV_HEADS, RWKV_HEAD
    n_pages = PAST_LEN // PAGE_SIZE
    n_used = DEC_BATCH * n_pages
    n_pool = n_used + n_used // 4
    perm = jax.random.permutation(nk(), n_pool).astype(jnp.int32)
    page_table = perm[:n_used].reshape(DEC_BATCH, n_pages)

    rwkv_vec = jnp.stack([
        jax.random.uniform(nk(), (N_A, D), f32, -5.0, -1.0),
        normal((N_A, D), 0.1),
        0.85 + normal((N_A, D), 0.02),
        1.0 + normal((N_A, D), 0.02),
        1.0 + normal((N_A, D), 0.02),
        normal((N_A, D), 0.02),
    ], axis=1)

    return {
        "x_prompt": normal((BATCH, SEQ, D), 1.0),
        "x_sample": normal((DEC_BATCH, DEC_SEQ, D), 1.0),
        "cache_k": normal((n_pool, PAGE_SIZE, ATT_HEADS, 2, ATT_HEAD), 1.0),
        "cache_v": normal((n_pool, PAGE_SIZE, ATT_HEADS, 2 * ATT_HEAD), 1.0),
        "state_shift": normal((N_A, DEC_BATCH, D), 1.0),
        "state_wkv": normal((N_A, DEC_BATCH, H, N, N), 0.1),
        "page_table": page_table,
        "norm_mix": gain((DEPTH, 2, D)),
        "norm_ffn": gain((DEPTH, 2, D)),
        "rwkv_mu": jax.random.uniform(nk(), (N_A, 6, D), f32),
        "rwkv_vec": rwkv_vec,
        "rwkv_wr": normal((N_A, D, D), D ** -0.5),
        "rwkv_wk": normal((N_A, D, D), D ** -0.5),
        "rwkv_wv": normal((N_A, D, D), D ** -0.5),
        "rwkv_wo": normal((N_A, D, D), D ** -0.5),
        "rwkv_w1": normal((N_A, D, D_DECAY_LORA), D ** -0.5),
        "rwkv_w2": normal((N_A, D_DECAY_LORA, D), 0.5 * D_DECAY_LORA ** -0.5),
        "rwkv_a1": normal((N_A, D, D_AAA_LORA), D ** -0.5),
        "rwkv_a2": normal((N_A, D_AAA_LORA, D), 0.5 * D_AAA_LORA ** -0.5),
        "rwkv_v0": normal((N_A - 1, D), 0.1),
        "rwkv_v1": normal((N_A - 1, D, D_MV_LORA), D ** -0.5),
        "rwkv_v2": normal((N_A - 1, D_MV_LORA, D), 0.5 * D_MV_LORA ** -0.5),
        "rwkv_g1": normal((N_A, D, D_GATE_LORA), D ** -0.5),
        "rwkv_g2": normal((N_A, D_GATE_LORA, D), D_GATE_LORA ** -0.5),
        "rwkv_rk": normal((N_A, H, N), 0.1),
        "kv_norm": gain((D,)),
        "kv_wk": normal((D, ATT_WIDTH), D ** -0.5),
        "kv_wv": normal((D, ATT_WIDTH), D ** -0.5),
        "attn_wq": normal((N_B, D, ATT_WIDTH), D ** -0.5),
        "attn_wo": normal((N_B, ATT_WIDTH, D), ATT_WIDTH ** -0.5),
        "attn_lambda": normal((N_B, 4, ATT_HEAD), 0.1),
        "attn_subln": gain((N_B, 2 * ATT_HEAD)),
        "ffn_w1": normal((DEPTH, D, D_FF), D ** -0.5),
        "ffn_w2": normal((DEPTH, D_FF, D), D_FF ** -0.5),
    }


def reference(x_prompt, x_sample, cache_k, cache_v, state_shift, state_wkv, page_table,
              norm_mix, norm_ffn, rwkv_mu, rwkv_vec, rwkv_wr, rwkv_wk, rwkv_wv, rwkv_wo,
              rwkv_w1, rwkv_w2, rwkv_a1, rwkv_a2, rwkv_v0, rwkv_v1, rwkv_v2, rwkv_g1, rwkv_g2,
              rwkv_rk, kv_norm, kv_wk, kv_wv, attn_wq, attn_wo, attn_lambda, attn_subln,
              ffn_w1, ffn_w2):

    def trunk(x, shift0, wkv0, pos, attend):
        Bsz, T, _ = x.shape
        new_shift, new_wkv = [], []
        v_first = None
        k_sh = None
        v_sh = None
        for l in range(DEPTH):
            xn = rms_norm(x, norm_mix[l, 0])
            if l < N_A:
                vres = None if l == 0 else (rwkv_v0[l - 1], rwkv_v1[l - 1], rwkv_v2[l - 1])
                p = (rwkv_mu[l], rwkv_vec[l], rwkv_wr[l], rwkv_wk[l], rwkv_wv[l], rwkv_wo[l],
                     rwkv_w1[l], rwkv_w2[l], rwkv_a1[l], rwkv_a2[l], rwkv_g1[l], rwkv_g2[l], rwkv_rk[l])
                h, sh, st, v_first = rwkv7_time_mix(xn, shift0[l], wkv0[l], v_first, p, vres)
                new_shift.append(sh)
                new_wkv.append(st)
            else:
                j = l - N_A
                q = rope((xn @ attn_wq[j]).reshape(Bsz, T, ATT_HEADS, 2, ATT_HEAD), pos)
                lp = attn_lambda[j].astype(jnp.float32)
                lam_i = lambda_init(l)
                lam = jnp.exp(jnp.sum(lp[0] * lp[1])) - jnp.exp(jnp.sum(lp[2] * lp[3])) + lam_i
                o = attend(q, k_sh, v_sh, lam.astype(x.dtype))
                o = rms_norm(o, attn_subln[j], SUBLN_EPS) * (1.0 - lam_i)
                h = o.reshape(Bsz, T, ATT_WIDTH) @ attn_wo[j]
            x = x + rms_norm(h, norm_mix[l, 1])
            x = x + rms_norm(sq_relu_mlp(rms_norm(x, norm_ffn[l, 0]), ffn_w1[l], ffn_w2[l]), norm_ffn[l, 1])
            if l == N_A - 1:
                kv_in = rms_norm(x, kv_norm)
                k_sh = rope((kv_in @ kv_wk).reshape(Bsz, T, ATT_HEADS, 2, ATT_HEAD), pos)
                v_sh = (kv_in @ kv_wv).reshape(Bsz, T, ATT_HEADS, 2 * ATT_HEAD)
        return x, k_sh, v_sh, jnp.stack(new_shift), jnp.stack(new_wkv)

    Bp, Tp, D = x_prompt.shape
    shift0_p = jnp.zeros((N_A, Bp, D), x_prompt.dtype)
    wkv0_p = jnp.zeros((N_A, Bp, RWKV_HEADS, RWKV_HEAD, RWKV_HEAD), state_wkv.dtype)
    pos_p = jnp.arange(Tp, dtype=jnp.int32)
    y_prompt, k_prompt, v_prompt, shift_prompt, wkv_prompt = trunk(
        x_prompt, shift0_p, wkv0_p, pos_p, diff_attn_prompt)

    Bd, n_pages = page_table.shape
    page = cache_k.shape[1]
    past_len = n_pages * page
    k_past = cache_k[page_table].reshape(Bd, past_len, ATT_HEADS, 2, ATT_HEAD)
    v_past = cache_v[page_table].reshape(Bd, past_len, ATT_HEADS, 2 * ATT_HEAD)
    pos_s = past_len + jnp.arange(x_sample.shape[1], dtype=jnp.int32)
    attend_s = lambda q, k, v, lam: diff_attn_sample(q, k, v, k_past, v_past, lam)
    y_sample, k_sample, v_sample, shift_sample, wkv_sample = trunk(
        x_sample, state_shift, state_wkv, pos_s, attend_s)

    return (y_prompt, y_sample, k_prompt, v_prompt, shift_prompt, wkv_prompt,
            k_sample, v_sample, shift_sample, wkv_sample)
```

```python
import contextlib
import math
import numpy as np
import concourse.bass as bass
import concourse.mybir as mybir
from concourse.bass_utils import run_bass_kernel_spmd

F32 = mybir.dt.float32
BF16 = mybir.dt.bfloat16
I32 = mybir.dt.int32
U32 = mybir.dt.uint32
AF = mybir.ActivationFunctionType
ALU = mybir.AluOpType
AX = mybir.AxisListType

D = 2048
KT = 16
NP = 2048
NS = 8
NTOK = NP + NS
TT = 512
LCH = 64
NPAGE = 128
PAGE = 128
NPOOL = 1280
GN_EPS = 64e-5
NORM_EPS = 1e-6
SUBLN_EPS = 1e-5
NEG = -30000.0
ESZ = {F32: 4, BF16: 2, I32: 4, U32: 4}

CFG = dict(layers=4, tiles=5, stop=None, npool=NPOOL)


class V:
    def __init__(self, ap, arena, lo, hi):
        self.ap, self.arena, self.lo, self.hi = ap, arena, lo, hi

    def __getitem__(self, k):
        return self.ap[k]


class Sched:
    def __init__(self, nc, es):
        self.nc = nc
        self.es = es
        self.eng = dict(pe=nc.tensor, act=nc.scalar, dve=nc.vector, pool=nc.gpsimd, sp=nc.sync)
        self.sem = {k: es.enter_context(nc.semaphore("s_" + k)) for k in self.eng}
        self.cnt = {k: 0 for k in self.eng}
        self.seen = {k: {} for k in self.eng}
        self.recs = {}
        self.dsem = {}
        self.ninst = 0

    def _deps(self, v, mode, out, e=None):
        lst = self.recs.get(v.arena)
        if not lst:
            return
        ps = v.arena == "ps"
        for (lo, hi, kind, tk) in lst:
            if lo < v.hi and v.lo < hi and (mode == "w" or kind == "w" or (ps and tk[0] != e)):
                out.add(tk)

    def _rec(self, v, mode, tk):
        lst = self.recs.setdefault(v.arena, [])
        if mode == "w":
            lst[:] = [r for r in lst if not (v.lo <= r[0] and r[1] <= v.hi)]
        else:
            lst[:] = [r for r in lst if not (r[2] == "r" and r[3][0] == tk[0] and v.lo <= r[0] and r[1] <= v.hi)]
        lst.append((v.lo, v.hi, mode, tk))

    def _wait(self, e, tk):
        key, val = tk
        if key in self.eng:
            if key == e == "pe":
                return
            sem = self.sem[key]
        else:
            sem, tot = self.dsem[key]
            val = tot
        if self.seen[e].get(key, 0) >= val:
            return
        self.eng[e].wait_ge(sem, val)
        self.seen[e][key] = val
        self.ninst += 1

    def op(self, e, fn, reads=(), writes=()):
        tks = set()
        for v in reads:
            self._deps(v, "r", tks, e)
        for v in writes:
            self._deps(v, "w", tks, e)
        for tk in sorted(tks, key=str):
            self._wait(e, tk)
        inst = fn(self.eng[e])
        self.cnt[e] += 1
        inst.then_inc(self.sem[e], 1)
        tk = (e, self.cnt[e])
        for v in reads:
            self._rec(v, "r", tk)
        for v in writes:
            self._rec(v, "w", tk)
        self.ninst += 1
        return inst

    def group(self, e, fns, reads=(), writes=()):
        tks = set()
        for v in reads:
            self._deps(v, "r", tks, e)
        for v in writes:
            self._deps(v, "w", tks, e)
        for tk in sorted(tks, key=str):
            self._wait(e, tk)
        inst = None
        for fn in fns:
            inst = fn(self.eng[e])
            self.ninst += 1
        self.cnt[e] += 1
        inst.then_inc(self.sem[e], 1)
        tk = (e, self.cnt[e])
        for v in reads:
            self._rec(v, "r", tk)
        for v in writes:
            self._rec(v, "w", tk)

    def _slot(self, key):
        if key not in self.dsem:
            s = self.es.enter_context(self.nc.semaphore("d%d" % len(self.dsem)))
            self.dsem[key] = [s, 0]
        return self.dsem[key]

    def dma(self, q, out_ap, in_ap, reads=(), writes=(), slot=None, fn=None):
        tks = set()
        for v in reads:
            self._deps(v, "r", tks, q)
        for v in writes:
            self._deps(v, "w", tks, q)
        for tk in sorted(tks, key=str):
            self._wait(q, tk)
        ds = self._slot(slot)
        if fn is None:
            inst = self.eng[q].dma_start(out=out_ap, in_=in_ap)
        else:
            inst = fn(self.eng[q])
        ds[1] += 16
        inst.then_inc(ds[0], 16)
        tk = (slot, ds[1])
        for v in reads:
            self._rec(v, "r", tk)
        for v in writes:
            self._rec(v, "w", tk)
        self.ninst += 1

    def finish(self, e="sp"):
        for key, (sem, tot) in self.dsem.items():
            if tot > 0 and self.seen[e].get(key, 0) < tot:
                self.eng[e].wait_ge(sem, tot)
        for k in self.eng:
            if k != e and self.cnt[k] > 0:
                self.eng[e].wait_ge(self.sem[k], self.cnt[k])


class StopBuild(Exception):
    pass


class Arena:
    def __init__(self, ap, name, nbytes):
        self.ap, self.name, self.nbytes, self.pos = ap, name, nbytes, 0

    def alloc(self, dtype, shape):
        esz = ESZ[dtype]
        n = int(np.prod(shape[1:]))
        nb = (n * esz + 63) // 64 * 64
        lo = self.pos
        self.pos += nb
        assert self.pos <= self.nbytes, (self.name, self.pos, self.nbytes)
        ap = self.ap[0:shape[0], lo // 4:(lo + nb) // 4]
        if dtype != F32:
            ap = ap.bitcast(dtype)
        ap = ap[:, 0:n]
        if len(shape) == 3:
            ap = ap.rearrange("p (a b) -> p a b", a=shape[1])
        elif len(shape) == 4:
            ap = ap.rearrange("p (a b c) -> p a b c", a=shape[1], b=shape[2])
        return V(ap, self.name, lo, lo + nb)


def sub(v, ap, frac_lo=None, frac_hi=None):
    lo = v.lo if frac_lo is None else v.lo + frac_lo
    hi = v.hi if frac_hi is None else v.lo + frac_hi
    return V(ap, v.arena, lo, hi)


VEC_NM, VEC_NF, VEC_MU, VEC_VC, VEC_V0, VEC_RK, VEC_KVN = 0, 8, 16, 28, 40, 41, 43
NV = 44

def _const_layout():
    lay = {}
    pos = 0
    for name, n in [("ident", 128), ("ones", 128), ("blk", 128), ("rot", 128), ("cmask", 128),
                    ("m4", 256), ("ms", 64), ("idrep", 1024), ("reset", 512), ("eps", 4),
                    ("smask", 8), ("selh", 8), ("dsel", 64), ("pidx", 1)]:
        lay[name] = (pos, n)
        pos += n
    return lay, pos


CL, NCONST = _const_layout()


def make_consts():
    c = np.zeros((128, NCONST), np.float32)

    def put(name, arr):
        o, n = CL[name]
        c[:arr.shape[0], o:o + n] = arr

    put("ident", np.eye(128, dtype=np.float32))
    put("ones", np.ones((128, 128), np.float32))
    blk = np.zeros((128, 128), np.float32)
    blk[:64, :64] = 1
    blk[64:, 64:] = 1
    put("blk", blk)
    rot = np.zeros((128, 128), np.float32)
    for m in range(64):
        rot[m + 64, m] = -1.0
        rot[m, m + 64] = 1.0
    put("rot", rot)
    qi = np.arange(128)[:, None]
    kj = np.arange(128)[None, :]
    put("cmask", np.where(kj <= qi, 0.0, NEG).astype(np.float32))
    j = np.arange(64)[:, None]
    t = np.arange(64)[None, :]
    strict = (j < t).astype(np.float32)
    incl = (j <= t).astype(np.float32)
    put("m4", np.tile(np.concatenate([strict, incl, strict, incl], axis=1), (2, 1)))
    put("ms", np.tile((t < j).astype(np.float32), (2, 1)))
    put("idrep", np.tile(np.eye(64, dtype=np.float32), (2, 16)))
    rs = np.ones((128, 512), np.float32)
    rs[:, ::64] = 0
    put("reset", rs)
    eps = np.zeros((128, 4), np.float32)
    eps[:, 0] = NORM_EPS
    eps[:, 1] = GN_EPS
    eps[:, 2] = SUBLN_EPS
    eps[:, 3] = 1e-12
    put("eps", eps)
    r = np.arange(128)
    q_of = r % 8
    h_of = (r % 64) // 8
    put("smask", np.where(np.arange(8)[None, :] <= q_of[:, None], 0.0, NEG).astype(np.float32))
    put("selh", (np.arange(8)[None, :] == h_of[:, None]).astype(np.float32))
    ds = np.zeros((128, 64), np.float32)
    ds[np.arange(128), np.arange(128) % 64] = 1.0
    put("dsel", ds)
    put("pidx", np.arange(128, dtype=np.float32)[:, None])
    return c


def vec_layout(v):
    return np.ascontiguousarray(np.asarray(v, np.float32).reshape(KT, 128).T)


def w_cols(w, cw=128):
    w = np.asarray(w, np.float32)
    din, dout = w.shape
    return np.ascontiguousarray(w.reshape(din // 128, 128, dout // cw, cw).transpose(2, 1, 0, 3))


def rope_tables():
    half = 64
    inv = np.power(np.float32(10000.0), -np.arange(half, dtype=np.float32) / np.float32(half)).astype(np.float32)
    pos = np.concatenate([np.arange(NP), 16384 + np.arange(NS)]).astype(np.float32)
    ang = (pos[None, :] * inv[:, None]).astype(np.float32)
    cos = np.cos(ang).astype(np.float32)
    sin = np.sin(ang).astype(np.float32)
    return np.ascontiguousarray(np.concatenate([cos, cos], 0)), np.ascontiguousarray(np.concatenate([sin, sin], 0))


def lambda_init(layer_idx):
    return 0.8 - 0.6 * math.exp(-0.3 * layer_idx)


def build_program(cfg=CFG):
    nc = bass.Bass("TRN2", target_bir_lowering=False)
    es = contextlib.ExitStack()

    def din(name, shape, dt=F32):
        return nc.dram_tensor(name, list(shape), dt, kind="ExternalInput").ap()

    def dout(name, shape, dt=F32):
        return nc.dram_tensor(name, list(shape), dt, kind="ExternalOutput").ap()

    def dint(name, shape, dt=F32):
        return nc.dram_tensor(name, list(shape), dt, kind="Internal").ap()

    x0 = din("x0", [KT, 128, NTOK])
    shift0 = din("shift0", [2, 128, KT])
    wkv0 = din("wkv0", [2, 128, KT, 64])
    ptab = din("ptab", [1, NPAGE], I32)
    cache_k = din("cache_k", [cfg["npool"] * PAGE, D])
    cache_v = din("cache_v", [cfg["npool"] * PAGE, D])
    consts_d = din("consts", [128, NCONST])
    vecs_d = din("vecs", [128, NV, KT])
    cos_d = din("cos", [128, NTOK])
    sin_d = din("sin", [128, NTOK])
    lam_d = din("lamrep", [128, 2, 512])
    subw_d = din("subw", [128, 2, 256])
    wr_d = din("wr", [2, KT, 128, KT, 128])
    wk_d = din("wk", [2, KT, 128, KT, 128])
    wv_d = din("wv", [2, KT, 128, KT, 128])
    wo_d = din("wo", [2, KT, 128, KT, 128])
    lw1_d = din("lw1", [2, 128, KT, 96])
    la1_d = din("la1", [2, 128, KT, 96])
    lv1_d = din("lv1", [1, 128, KT, 64])
    lg1_d = din("lg1", [2, 128, KT, 256])
    lw2_d = din("lw2", [2, 96, D])
    la2_d = din("la2", [2, 96, D])
    lv2_d = din("lv2", [1, 64, D])
    lg2_d = din("lg2", [2, 128, 2, D])
    kvk_d = din("kvk", [KT, 128, KT, 128])
    kvv_d = din("kvv", [4, 128, KT, 512])
    aq_d = din("aq", [2, KT, 128, KT, 128])
    ao_d = din("ao", [2, KT, 128, KT, 128])
    f1_d = din("f1", [4, 64, 128, KT, 128])
    f2_d = din("f2", [4, KT, 128, 64, 128])

    yT = dout("yT", [KT, 128, NTOK])
    kT = dout("kT", [KT, 128, NTOK])
    vO = dout("vO", [NTOK, D])
    shO = dout("shO", [2, 2, 128, KT])
    wkvO = dout("wkvO", [2, 2, 128, KT, 64])

    X = dint("Xs", [KT, 128, NTOK])
    VF = dint("VFs", [KT, 128, NTOK])
    KTs = dint("KTs", [KT, 128, NTOK], BF16)
    Vs = dint("Vss", [NTOK, D], BF16)

    with es:
        S = Sched(nc, es)
        SBN = 206 * 1024
        sb_t = es.enter_context(nc.sbuf_tensor("sb", [128, SBN // 4], F32))
        ps_t = es.enter_context(nc.psum_tensor("ps", [128, 4096], F32))
        sb = Arena(sb_t[:, :], "sb", SBN)

        def psb(bank, nb=1):
            return V(ps_t[:, bank * 512:(bank + nb) * 512], "ps", bank * 2048, (bank + nb) * 2048)

        def psb16(bank):
            return V(ps_t[:, bank * 512:(bank + 1) * 512].bitcast(BF16), "ps", bank * 2048, (bank + 1) * 2048)

        def dres(name, lo=0, hi=1 << 30):
            return V(None, name, lo, hi)

        CON = sb.alloc(F32, [128, NCONST])
        VEC = sb.alloc(F32, [128, NV, KT])
        OMK = sb.alloc(F32, [128, 2, KT])
        IDB = sb.alloc(BF16, [128, 128])
        ONB = sb.alloc(BF16, [128, 128])
        BLB = sb.alloc(BF16, [128, 128])
        ROB = sb.alloc(BF16, [128, 128])
        M4B = sb.alloc(BF16, [128, 4, 64])
        MSB = sb.alloc(BF16, [128, 64])
        IDR = sb.alloc(BF16, [128, 8, 64])
        ST = sb.alloc(F32, [128, KT, 64])
        STB = sb.alloc(BF16, [128, KT, 64])
        CAR = sb.alloc(F32, [128, KT])
        LAMV = sb.alloc(F32, [128, 8])
        SUBW = sb.alloc(F32, [128, 2, 256])
        SQ = [sb.alloc(BF16, [128, TT]) for _ in range(2)]
        RSTD = sb.alloc(F32, [128, TT])
        COS = sb.alloc(F32, [128, TT])
        SIN = sb.alloc(F32, [128, TT])
        base_pos = sb.pos
        cur = {"NA": TT}

        dbg_list = []

        def dbg(name, v, ap, shape, dt=F32):
            if not cfg.get("debug"):
                return
            d = nc.dram_tensor("dbg_" + name, list(shape), dt, kind="ExternalOutput").ap()
            S.dma("sp", d, ap, reads=[v], slot="dbg")

        def cc(name, p=128):
            o, n = CL[name]
            return CON[0:p, o:o + n]

        S.dma("sp", CON[:, :], consts_d, writes=[CON], slot="con")
        S.dma("sp", VEC[:, :, :], vecs_d, writes=[VEC], slot="vec")
        S.dma("sp", SUBW[:, :, :], subw_d, writes=[SUBW], slot="subw")
        for dst, name in [(IDB, "ident"), (ONB, "ones"), (BLB, "blk"), (ROB, "rot")]:
            S.op("dve", lambda e, dst=dst, name=name: e.tensor_copy(out=dst[:, :], in_=cc(name)), reads=[CON], writes=[dst])
        S.op("dve", lambda e: e.tensor_copy(out=M4B[:, :, :], in_=cc("m4").rearrange("p (a b) -> p a b", a=4)), reads=[CON], writes=[M4B])
        S.op("dve", lambda e: e.tensor_copy(out=MSB[:, :], in_=cc("ms")), reads=[CON], writes=[MSB])
        S.op("dve", lambda e: e.tensor_copy(out=IDR[:, :, :], in_=cc("idrep")[:, 0:512].rearrange("p (a b) -> p a b", a=8)), reads=[CON], writes=[IDR])
        for l in range(2):
            S.op("dve", lambda e, l=l: e.tensor_scalar(out=OMK[:, l, :], in0=VEC[:, VEC_VC + l * 6 + 3, :], scalar1=-1.0, scalar2=1.0,
                                                    op0=ALU.mult, op1=ALU.add), reads=[VEC], writes=[OMK])

        def vcol(idx, kt):
            return VEC[:, idx, kt:kt + 1]

        EPS_N = cc("eps")[:, 0:1]
        EPS_G = cc("eps")[:, 1:2]
        EPS_S = cc("eps")[:, 2:3]
        EPS_K = cc("eps")[:, 3:4]

        PB = [psb(i) for i in range(8)]

        def rms_stats(src, n, pbank, scale, eps_ap, nk=KT, blockones=False, srcs=None):
            lhs = BLB if blockones else ONB
            for kt in range(nk):
                sq = SQ[kt % 2]
                S.op("act", lambda e, kt=kt, sq=sq: e.activation(out=sq[:, 0:n], in_=src[:, kt, 0:n], func=AF.Square),
                     reads=[src], writes=[sq])
                S.op("pe", lambda e, kt=kt, sq=sq: e.matmul(pbank[:, 0:n], lhsT=lhs[:, :], rhs=sq[:, 0:n], start=(kt == 0), stop=(kt == nk - 1)),
                     reads=[sq, lhs], writes=[pbank])
            S.op("act", lambda e: e.activation(out=RSTD[:, 0:n], in_=pbank[:, 0:n], func=AF.Sqrt, bias=eps_ap, scale=scale),
                 reads=[pbank, CON], writes=[RSTD])
            S.op("dve", lambda e: e.reciprocal(out=RSTD[:, 0:n], in_=RSTD[:, 0:n]), reads=[RSTD], writes=[RSTD])

        def rms_apply(dst, src, gidx, n, doff=0):
            for kt in range(KT):
                S.op("dve", lambda e, kt=kt: e.scalar_tensor_tensor(out=dst[:, kt, doff:doff + n], in0=src[:, kt, 0:n], scalar=vcol(gidx, kt),
                                                                    in1=RSTD[:, 0:n], op0=ALU.mult, op1=ALU.mult),
                     reads=[src, RSTD, VEC], writes=[dst])

        def load_x(dst, l, t0, n):
            srcd = x0 if l == 0 else X
            S.dma("sp", dst[:, :, 0:n], srcd[:, :, t0:t0 + n].rearrange("k p t -> p k t"),
                  reads=[dres("X", t0, t0 + n)] if l > 0 else [], writes=[dst], slot="ldx")

        def wload(dst_v, src_ap, slot):
            S.dma("pool", None, None, writes=[dst_v], slot=slot,
                  fn=lambda e: e.dma_start(out=dst_v.ap, in_=src_ap, max_dma_last_dim=8192))

        def proj_fm(pbank, wv, rhs, n, nk=KT):
            S.group("pe", [lambda e, kt=kt: e.matmul(pbank[:, 0:n], lhsT=wv[:, kt, :], rhs=rhs[:, kt, 0:n], start=(kt == 0), stop=(kt == nk - 1))
                           for kt in range(nk)], reads=[wv, rhs], writes=[pbank])

        def rope_evac(pbank, n, t0, out_f32, tmpb, pb2):
            S.op("act", lambda e: e.activation(out=tmpb[:, 0:n], in_=pbank[:, 0:n], func=AF.Copy), reads=[pbank], writes=[tmpb])
            S.op("pe", lambda e: e.matmul(pb2[:, 0:n], lhsT=ROB[:, :], rhs=tmpb[:, 0:n], start=True, stop=True), reads=[ROB, tmpb], writes=[pb2])
            S.op("dve", lambda e: e.tensor_tensor(out=out_f32[:, 0:n], in0=pbank[:, 0:n], in1=COS[:, 0:n], op=ALU.mult),
                 reads=[pbank, COS], writes=[out_f32])
            S.op("dve", lambda e: e.tensor_tensor(out=tmpb[:, 0:n], in0=pb2[:, 0:n], in1=SIN[:, 0:n], op=ALU.mult),
                 reads=[pb2, SIN], writes=[tmpb])
            S.op("dve", lambda e: e.tensor_tensor(out=out_f32[:, 0:n], in0=out_f32[:, 0:n], in1=tmpb[:, 0:n], op=ALU.add),
                 reads=[tmpb, out_f32], writes=[out_f32])

        def load_rope(t0, n):
            S.dma("sp", COS[:, 0:n], cos_d[:, t0:t0 + n], writes=[COS], slot="rope")
            S.dma("sp", SIN[:, 0:n], sin_d[:, t0:t0 + n], writes=[SIN], slot="rope")

        def post_and_ffn(l, XT, HO, n, t0, ffn_pos):
            NA = cur["NA"]
            sb.pos = ffn_pos
            XNB = sb.alloc(BF16, [128, KT, NA])
            kv_pos = sb.pos
            H = sb.alloc(BF16, [128, 32, NA])
            R1 = [sb.alloc(BF16, [128, NA]) for _ in range(2)]
            W1 = [sb.alloc(BF16, [128, 4, KT, 128]) for _ in range(2)]
            W2 = [sb.alloc(BF16, [128, 32, 128]) for _ in range(2)]
            rms_stats(HO, n, PB[7], 1.0 / D, EPS_N)
            rms_apply(HO, HO, VEC_NM + l * 2 + 1, n)
            S.op("dve", lambda e: e.tensor_tensor(out=XT[:, :, 0:n], in0=XT[:, :, 0:n], in1=HO[:, :, 0:n], op=ALU.add),
                 reads=[XT, HO], writes=[XT])
            rms_stats(XT, n, PB[7], 1.0 / D, EPS_N)
            rms_apply(XNB, XT, VEC_NF + l * 2 + 0, n)
            for half in range(2):
                for g in range(8):
                    w1 = W1[g % 2]
                    wload(w1, f1_d[l, half * 32 + g * 4:half * 32 + g * 4 + 4].rearrange("f p k c -> p f k c"), ("w1", g % 2))
                    for f in range(4):
                        fc = g * 4 + f
                        pb = PB[fc % 2]
                        S.group("pe", [lambda e, kt=kt, f=f, w1=w1, pb=pb: e.matmul(pb[:, 0:n], lhsT=w1[:, f, kt, :], rhs=XNB[:, kt, 0:n],
                                                                            start=(kt == 0), stop=(kt == KT - 1)) for kt in range(KT)],
                                reads=[w1, XNB], writes=[pb])
                        r1 = R1[fc % 2]
                        S.op("act", lambda e, pb=pb, r1=r1: e.activation(out=r1[:, 0:n], in_=pb[:, 0:n], func=AF.Relu), reads=[pb], writes=[r1])
                        S.op("pool", lambda e, r1=r1, fc=fc: e.tensor_tensor(out=H[:, fc, 0:n], in0=r1[:, 0:n], in1=r1[:, 0:n], op=ALU.mult),
                             reads=[r1], writes=[sub(H, None, fc * NA * 2, (fc + 1) * NA * 2)])
                for oc in range(KT):
                    w2 = W2[oc % 2]
                    wload(w2, f2_d[l, oc, :, half * 32:half * 32 + 32, :], ("w2", oc % 2))
                    pb = PB[2 + oc % 2]
                    S.group("pe", [lambda e, fk=fk, w2=w2, pb=pb: e.matmul(pb[:, 0:n], lhsT=w2[:, fk, :], rhs=H[:, fk, 0:n],
                                                                        start=(fk == 0), stop=(fk == 31)) for fk in range(32)],
                            reads=[w2, H], writes=[pb])
                    hov = sub(HO, None, oc * NA * 4, (oc + 1) * NA * 4)
                    if half == 0:
                        S.op("act", lambda e, pb=pb, oc=oc: e.activation(out=HO[:, oc, 0:n], in_=pb[:, 0:n], func=AF.Copy), reads=[pb], writes=[hov])
                    else:
                        S.op("dve", lambda e, pb=pb, oc=oc: e.tensor_tensor(out=HO[:, oc, 0:n], in0=pb[:, 0:n], in1=HO[:, oc, 0:n], op=ALU.add),
                             reads=[pb, hov], writes=[hov])
            rms_stats(HO, n, PB[7], 1.0 / D, EPS_N)
            rms_apply(HO, HO, VEC_NF + l * 2 + 1, n)
            S.op("dve", lambda e: e.tensor_tensor(out=XT[:, :, 0:n], in0=XT[:, :, 0:n], in1=HO[:, :, 0:n], op=ALU.add),
                 reads=[XT, HO], writes=[XT])
            dst = yT if l == 3 else X
            S.dma("sp", dst[:, :, t0:t0 + n].rearrange("k p t -> p k t"), XT[:, :, 0:n], reads=[XT],
                  writes=[dres("X", t0, t0 + n)] if l < 3 else [], slot="stx")
            if cfg.get("stop") == "ffn" and l == cfg.get("stop_layer", 0):
                raise StopBuild()
            if l == 1:
                kv_proj(XT, XNB, n, t0, kv_pos)

        def kv_proj(XT, XNB, n, t0, kv_pos):
            NA = cur["NA"]
            mark = sb.pos
            sb.pos = kv_pos
            WK = [sb.alloc(BF16, [128, KT, 128]) for _ in range(2)]
            WVV = sb.alloc(BF16, [128, KT, 512])
            KF = [sb.alloc(F32, [128, NA]) for _ in range(2)]
            KB = [sb.alloc(BF16, [128, NA]) for _ in range(2)]
            TB = sb.alloc(BF16, [128, NA])
            VT = [sb.alloc(F32, [128, 512]) for _ in range(2)]
            VTB = [sb.alloc(BF16, [128, 512]) for _ in range(2)]
            rms_stats(XT, n, PB[7], 1.0 / D, EPS_N)
            rms_apply(XNB, XT, VEC_KVN, n)
            load_rope(t0, n)
            if cfg.get("stop") == "kv0":
                raise StopBuild()
            for hm in range(KT):
                wk = WK[hm % 2]
                wload(wk, kvk_d[hm], ("wkv", hm % 2))
                pb = PB[hm % 2]
                proj_fm(pb, wk, XNB, n)
                kf, kb = KF[hm % 2], KB[hm % 2]
                if cfg.get("stop") == "kv1":
                    raise StopBuild()
                rope_evac(pb, n, t0, kf, TB, PB[2 + hm % 2])
                if cfg.get("stop") == "kv2":
                    raise StopBuild()
                S.op("act", lambda e, kf=kf, kb=kb: e.activation(out=kb[:, 0:n], in_=kf[:, 0:n], func=AF.Copy), reads=[kf], writes=[kb])
                S.dma("sp", kT[hm, :, t0:t0 + n], kf[:, 0:n], reads=[kf], slot=("kfo", hm % 2))
                S.dma("sp", KTs[hm, :, t0:t0 + n], kb[:, 0:n], reads=[kb], writes=[dres("KTs", hm * 4096 + t0, hm * 4096 + t0 + n)], slot=("kbo", hm % 2))
            if cfg.get("stop") == "kvk":
                raise StopBuild()
            ntt = (n + 127) // 128
            for fcg in range(4):
                wload(WVV, kvv_d[fcg], "wvv")
                for tt in range(ntt):
                    m = min(128, n - tt * 128)
                    i = (fcg * ntt + tt) % 2
                    pb = PB[4 + i]
                    S.group("pe", [lambda e, kt=kt, tt=tt, m=m, pb=pb: e.matmul(pb[0:m, :], lhsT=XNB[:, kt, tt * 128:tt * 128 + m], rhs=WVV[:, kt, :],
                                                                             start=(kt == 0), stop=(kt == KT - 1)) for kt in range(KT)],
                            reads=[XNB, WVV], writes=[pb])
                    vt, vtb = VT[i], VTB[i]
                    S.op("act", lambda e, vt=vt, pb=pb, m=m: e.activation(out=vt[0:m, :], in_=pb[0:m, :], func=AF.Copy), reads=[pb], writes=[vt])
                    S.op("dve", lambda e, vt=vt, vtb=vtb, m=m: e.tensor_copy(out=vtb[0:m, :], in_=vt[0:m, :]), reads=[vt], writes=[vtb])
                    r0 = t0 + tt * 128
                    S.dma("sp", vO[r0:r0 + m, fcg * 512:(fcg + 1) * 512], vt[0:m, :], reads=[vt], slot=("vfo", i))
                    S.dma("sp", Vs[r0:r0 + m, fcg * 512:(fcg + 1) * 512], vtb[0:m, :], reads=[vtb], writes=[dres("Vs", r0 * 4 + fcg, r0 * 4 + fcg + 1)], slot=("vbo", i))
            sb.pos = mark

        def rwkv_tile(l, t0, n, first, last, sample):
            Lc = 8 if sample else LCH
            nch = n // Lc
            nlev = int(math.log2(Lc)) - 1
            NA = NS if sample else TT
            cur["NA"] = NA
            sb.pos = base_pos
            XT = sb.alloc(F32, [128, KT, NA])
            region_pos = sb.pos
            XN = sb.alloc(F32, [128, KT, NA + 1])
            XXT = [sb.alloc(F32, [128, NA]) for _ in range(2)]
            regA_end = sb.pos
            TMPB = sb.alloc(BF16, [128, KT, NA])
            MIX = [sb.alloc(BF16, [128, KT, NA]) for _ in range(3)]
            HW = sb.alloc(BF16, [96, NA])
            HA = sb.alloc(BF16, [96, NA])
            HV = sb.alloc(BF16, [64, NA])
            HG = sb.alloc(BF16, [128, 2, NA])
            l1_pos = sb.pos
            L1 = [sb.alloc(BF16, [128, KT, 96]), sb.alloc(BF16, [128, KT, 96]), sb.alloc(BF16, [128, KT, 64]), sb.alloc(BF16, [128, KT, 256])]
            sb.pos = l1_pos
            WCH = [[sb.alloc(BF16, [128, KT, 128]) for _ in range(3)] for _ in range(2)]
            L2C = [[sb.alloc(BF16, [96, 128]), sb.alloc(BF16, [96, 128]), sb.alloc(BF16, [64, 128]), sb.alloc(BF16, [128, 2, 128])] for _ in range(2)]
            p2_pos = sb.pos

            load_x(XT, l, t0, n)
            if first:
                if sample:
                    S.dma("sp", CAR[:, :], shift0[l], writes=[CAR], slot="car")
                    S.dma("sp", ST[:, :, :], wkv0[l], writes=[ST], slot="st0")
                    S.op("act", lambda e: e.activation(out=STB[:, :, :], in_=ST[:, :, :], func=AF.Copy), reads=[ST], writes=[STB])
                else:
                    S.op("pool", lambda e: e.memset(CAR[:, :], 0.0), writes=[CAR])
                    S.op("pool", lambda e: e.memset(ST[:, :, :], 0.0), writes=[ST])
                    S.op("pool", lambda e: e.memset(STB[:, :, :], 0.0), writes=[STB])
            for i, (dsrc, dst) in enumerate([(lw1_d[l], L1[0]), (la1_d[l], L1[1]), (lv1_d[0], L1[2]), (lg1_d[l], L1[3])]):
                if i == 2 and l == 0:
                    continue
                wload(dst, dsrc, ("l1", i))
            rms_stats(XT, n, PB[7], 1.0 / D, EPS_N)
            S.op("dve", lambda e: e.tensor_copy(out=XN[:, :, 0:1], in_=CAR[:, :].unsqueeze(2)), reads=[CAR], writes=[XN])
            rms_apply(XN, XT, VEC_NM + l * 2 + 0, n, doff=1)
            S.op("dve", lambda e: e.tensor_copy(out=CAR[:, :].unsqueeze(2), in_=XN[:, :, n:n + 1]), reads=[XN], writes=[CAR])
            if last:
                S.dma("sp", shO[l, 1 if sample else 0], CAR[:, :], reads=[CAR], slot="sho")

            def mix(i, dst):
                for kt in range(KT):
                    xx = XXT[kt % 2]
                    S.op("pool", lambda e, kt=kt, xx=xx: e.tensor_tensor(out=xx[:, 0:n], in0=XN[:, kt, 0:n], in1=XN[:, kt, 1:n + 1], op=ALU.subtract),
                         reads=[XN], writes=[xx])
                    S.op("dve", lambda e, kt=kt, xx=xx: e.scalar_tensor_tensor(out=dst[:, kt, 0:n], in0=xx[:, 0:n], scalar=vcol(VEC_MU + l * 6 + i, kt),
                                                                               in1=XN[:, kt, 1:n + 1], op0=ALU.mult, op1=ALU.add),
                         reads=[xx, XN, VEC], writes=[dst])

            def lora1(src, w, r, outv, func, pb, oc=None):
                cs = slice(0, r) if oc is None else slice(oc * 128, oc * 128 + 128)
                rr = r if oc is None else 128
                S.group("pe", [lambda e, kt=kt: e.matmul(pb[0:rr, 0:n], lhsT=w[:, kt, cs], rhs=src[:, kt, 0:n], start=(kt == 0), stop=(kt == KT - 1))
                               for kt in range(KT)], reads=[w, src], writes=[pb])
                oap = outv[0:rr, 0:n] if oc is None else outv[:, oc, 0:n]
                S.op("act", lambda e: e.activation(out=oap, in_=pb[0:rr, 0:n], func=func), reads=[pb], writes=[outv])

            mix(1, TMPB)
            lora1(TMPB, L1[0], 96, HW, AF.Tanh, PB[0])
            mix(4, TMPB)
            lora1(TMPB, L1[1], 96, HA, AF.Copy, PB[1])
            mix(5, TMPB)
            lora1(TMPB, L1[3], 256, HG, AF.Sigmoid, PB[0], oc=0)
            lora1(TMPB, L1[3], 256, HG, AF.Sigmoid, PB[1], oc=1)
            mix(0, MIX[0])
            mix(2, MIX[1])
            mix(3, MIX[2])
            if l == 1:
                lora1(MIX[2], L1[2], 64, HV, AF.Copy, PB[0])
            YG = TMPB
            if cfg.get("stop") == "p1" and l == cfg.get("stop_layer", 0):
                raise StopBuild()

            sb.pos = p2_pos if sample else base_pos
            Rf = sb.alloc(F32, [128, NA]); Kf = sb.alloc(F32, [128, NA]); Vf = sb.alloc(F32, [128, NA])
            LW = sb.alloc(F32, [128, NA]); AS = sb.alloc(F32, [128, NA]); KK = sb.alloc(F32, [128, NA])
            T1 = sb.alloc(F32, [128, NA]); T2 = sb.alloc(F32, [128, NA]); CLs = sb.alloc(F32, [128, NA])
            E1 = sb.alloc(F32, [128, NA]); E2 = sb.alloc(F32, [128, NA]); Gf = sb.alloc(F32, [128, NA])
            Cc = sb.alloc(F32, [128, NA]); Yf = sb.alloc(F32, [128, NA]); VF1 = sb.alloc(F32, [128, NA])
            T3B = sb.alloc(BF16, [128, NA])
            AR = sb.alloc(BF16, [128, nch, 2, Lc]); BK = sb.alloc(BF16, [128, nch, 2, Lc]); VB = sb.alloc(BF16, [128, NA])
            TM = sb.alloc(BF16, [128, nch, 3, 64])
            MTS = sb.alloc(BF16, [128, nch, 4, Lc])
            PT0 = sb.alloc(BF16, [128, nch, Lc])
            PP = [sb.alloc(BF16, [128, nch, Lc]) for _ in range(2)]
            PPT = [sb.alloc(BF16, [128, nch, Lc]) for _ in range(2)]
            WW = [sb.alloc(BF16, [128, nch, Lc]) for _ in range(2)]
            XS = sb.alloc(BF16, [128, 64]); US = sb.alloc(BF16, [128, 64])
            X2S = sb.alloc(BF16, [128, nch, 64])
            assert sb.pos <= regA_end or sample, (sb.pos, regA_end)

            vc = lambda i, hp: VEC[:, VEC_VC + l * 6 + i, hp:hp + 1]

            def load_hp(hp):
                s = hp % 2
                wload(WCH[s][0], wr_d[l, hp], ("wch", s))
                wload(WCH[s][1], wk_d[l, hp], ("wch", s))
                wload(WCH[s][2], wv_d[l, hp], ("wch", s))
                S.dma("pool", L2C[s][0][:, :], lw2_d[l, :, hp * 128:(hp + 1) * 128], writes=[L2C[s][0]], slot=("l2c", s))
                S.dma("pool", L2C[s][1][:, :], la2_d[l, :, hp * 128:(hp + 1) * 128], writes=[L2C[s][1]], slot=("l2c", s))
                if l == 1:
                    S.dma("pool", L2C[s][2][:, :], lv2_d[0, :, hp * 128:(hp + 1) * 128], writes=[L2C[s][2]], slot=("l2c", s))
                S.dma("pool", L2C[s][3][:, :, :], lg2_d[l, :, :, hp * 128:(hp + 1) * 128], writes=[L2C[s][3]], slot=("l2c", s))

            load_hp(0)
            for hp in range(KT):
                s = hp % 2
                if hp + 1 < KT:
                    load_hp(hp + 1)
                wr, wk, wv = WCH[s]
                w2c, a2c, v2c, g2c = L2C[s]
                proj_fm(PB[0], wr, MIX[0], n)
                S.op("act", lambda e: e.activation(out=Rf[:, 0:n], in_=PB[0][:, 0:n], func=AF.Copy), reads=[PB[0]], writes=[Rf])
                proj_fm(PB[1], wk, MIX[1], n)
                S.op("act", lambda e: e.activation(out=Kf[:, 0:n], in_=PB[1][:, 0:n], func=AF.Copy), reads=[PB[1]], writes=[Kf])
                proj_fm(PB[0], wv, MIX[2], n)
                S.op("act", lambda e: e.activation(out=Vf[:, 0:n], in_=PB[0][:, 0:n], func=AF.Copy), reads=[PB[0]], writes=[Vf])
                S.op("pe", lambda e: e.matmul(PB[1][:, 0:n], lhsT=w2c[:, :], rhs=HW[:, 0:n], start=True, stop=True), reads=[w2c, HW], writes=[PB[1]])
                S.op("act", lambda e, hp=hp: e.activation(out=LW[:, 0:n], in_=PB[1][:, 0:n], func=AF.Sigmoid, bias=vc(0, hp), scale=1.0),
                     reads=[PB[1], VEC], writes=[LW])
                S.op("dve", lambda e: e.tensor_scalar(out=LW[:, 0:n], in0=LW[:, 0:n], scalar1=-math.exp(-0.5), scalar2=None, op0=ALU.mult),
                     reads=[LW], writes=[LW])
                S.op("pe", lambda e: e.matmul(PB[0][:, 0:n], lhsT=a2c[:, :], rhs=HA[:, 0:n], start=True, stop=True), reads=[a2c, HA], writes=[PB[0]])
                S.op("act", lambda e, hp=hp: e.activation(out=AS[:, 0:n], in_=PB[0][:, 0:n], func=AF.Sigmoid, bias=vc(1, hp), scale=1.0),
                     reads=[PB[0], VEC], writes=[AS])
                S.group("pe", [lambda e, j=j: e.matmul(PB[1][:, 0:n], lhsT=g2c[:, j, :], rhs=HG[:, j, 0:n], start=(j == 0), stop=(j == 1)) for j in range(2)],
                        reads=[g2c, HG], writes=[PB[1]])
                S.op("act", lambda e: e.activation(out=Gf[:, 0:n], in_=PB[1][:, 0:n], func=AF.Copy), reads=[PB[1]], writes=[Gf])
                vfd = VF[hp, :, t0:t0 + n]
                if l == 0:
                    S.dma("sp", vfd, Vf[:, 0:n], reads=[Vf], writes=[dres("VF", hp, hp + 1)], slot="vfst")
                else:
                    S.dma("sp", VF1[:, 0:n], vfd, reads=[dres("VF", hp, hp + 1)], writes=[VF1], slot="vfld")
                    S.op("pe", lambda e: e.matmul(PB[0][:, 0:n], lhsT=v2c[:, :], rhs=HV[:, 0:n], start=True, stop=True), reads=[v2c, HV], writes=[PB[0]])
                    S.op("act", lambda e, hp=hp: e.activation(out=T1[:, 0:n], in_=PB[0][:, 0:n], func=AF.Sigmoid, bias=VEC[:, VEC_V0, hp:hp + 1], scale=1.0),
                         reads=[PB[0], VEC], writes=[T1])
                    S.op("dve", lambda e: e.tensor_tensor(out=T2[:, 0:n], in0=VF1[:, 0:n], in1=Vf[:, 0:n], op=ALU.subtract), reads=[VF1, Vf], writes=[T2])
                    S.op("dve", lambda e: e.tensor_tensor(out=T2[:, 0:n], in0=T2[:, 0:n], in1=T1[:, 0:n], op=ALU.mult), reads=[T2, T1], writes=[T2])
                    S.op("dve", lambda e: e.tensor_tensor(out=Vf[:, 0:n], in0=Vf[:, 0:n], in1=T2[:, 0:n], op=ALU.add), reads=[Vf, T2], writes=[Vf])
                S.op("dve", lambda e, hp=hp: e.tensor_scalar(out=KK[:, 0:n], in0=Kf[:, 0:n], scalar1=vc(2, hp), scalar2=None, op0=ALU.mult),
                     reads=[Kf, VEC], writes=[KK])
                S.op("act", lambda e: e.activation(out=T3B[:, 0:n], in_=KK[:, 0:n], func=AF.Square), reads=[KK], writes=[T3B])
                S.op("pe", lambda e: e.matmul(PB[0][:, 0:n], lhsT=BLB[:, :], rhs=T3B[:, 0:n], start=True, stop=True), reads=[BLB, T3B], writes=[PB[0]])
                S.op("act", lambda e: e.activation(out=T1[:, 0:n], in_=PB[0][:, 0:n], func=AF.Sqrt), reads=[PB[0]], writes=[T1])
                S.op("dve", lambda e: e.tensor_scalar(out=T1[:, 0:n], in0=T1[:, 0:n], scalar1=1e-12, scalar2=None, op0=ALU.max), reads=[T1], writes=[T1])
                S.op("dve", lambda e: e.reciprocal(out=T1[:, 0:n], in_=T1[:, 0:n]), reads=[T1], writes=[T1])
                S.op("dve", lambda e: e.tensor_tensor(out=KK[:, 0:n], in0=KK[:, 0:n], in1=T1[:, 0:n], op=ALU.mult), reads=[KK, T1], writes=[KK])
                S.op("dve", lambda e, hp=hp: e.tensor_scalar(out=T1[:, 0:n], in0=AS[:, 0:n], scalar1=vc(3, hp), scalar2=OMK[:, l, hp:hp + 1],
                                                            op0=ALU.mult, op1=ALU.add), reads=[AS, VEC, OMK], writes=[T1])
                S.op("dve", lambda e: e.tensor_tensor(out=Kf[:, 0:n], in0=Kf[:, 0:n], in1=T1[:, 0:n], op=ALU.mult), reads=[Kf, T1], writes=[Kf])
                S.op("dve", lambda e, hp=hp: e.scalar_tensor_tensor(out=T3B[:, 0:n], in0=Rf[:, 0:n], scalar=VEC[:, VEC_RK + l, hp:hp + 1], in1=Kf[:, 0:n],
                                                                   op0=ALU.mult, op1=ALU.mult), reads=[Rf, Kf, VEC], writes=[T3B])
                S.op("pe", lambda e: e.matmul(PB[1][:, 0:n], lhsT=BLB[:, :], rhs=T3B[:, 0:n], start=True, stop=True), reads=[BLB, T3B], writes=[PB[1]])
                S.op("act", lambda e: e.activation(out=Cc[:, 0:n], in_=PB[1][:, 0:n], func=AF.Copy), reads=[PB[1]], writes=[Cc])
                rst = cc("reset")[:, 0:n] if not sample else cc("reset")[:, 0:n]
                S.op("dve", lambda e: e.tensor_tensor_scan(out=CLs[:, 0:n], data0=rst, data1=LW[:, 0:n], initial=0.0, op0=ALU.mult, op1=ALU.add),
                     reads=[LW, CON], writes=[CLs])
                S.op("act", lambda e: e.activation(out=E1[:, 0:n], in_=CLs[:, 0:n], func=AF.Exp), reads=[CLs], writes=[E1])
                S.op("act", lambda e: e.activation(out=E2[:, 0:n], in_=CLs[:, 0:n], func=AF.Exp, scale=-1.0), reads=[CLs], writes=[E2])
                S.op("dve", lambda e: e.tensor_tensor(out=T1[:, 0:n], in0=CLs[:, 0:n], in1=LW[:, 0:n], op=ALU.subtract), reads=[CLs, LW], writes=[T1])
                S.op("act", lambda e: e.activation(out=T1[:, 0:n], in_=T1[:, 0:n], func=AF.Exp), reads=[T1], writes=[T1])
                v3 = lambda ap: ap.rearrange("p (c t) -> p c t", t=Lc)
                S.op("dve", lambda e: e.tensor_tensor(out=AR[:, :, 1, :], in0=v3(Rf[:, 0:n]), in1=v3(E1[:, 0:n]), op=ALU.mult), reads=[Rf, E1], writes=[AR])
                S.op("dve", lambda e: e.scalar_tensor_tensor(out=AR[:, :, 0, :], in0=v3(KK[:, 0:n]), scalar=-1.0, in1=v3(T1[:, 0:n]), op0=ALU.mult, op1=ALU.mult),
                     reads=[KK, T1], writes=[AR])
                S.op("dve", lambda e: e.tensor_tensor(out=T2[:, 0:n], in0=KK[:, 0:n], in1=AS[:, 0:n], op=ALU.mult), reads=[KK, AS], writes=[T2])
                S.op("dve", lambda e: e.tensor_tensor(out=BK[:, :, 0, :], in0=v3(T2[:, 0:n]), in1=v3(E2[:, 0:n]), op=ALU.mult), reads=[T2, E2], writes=[BK])
                S.op("dve", lambda e: e.tensor_tensor(out=BK[:, :, 1, :], in0=v3(Kf[:, 0:n]), in1=v3(E2[:, 0:n]), op=ALU.mult), reads=[Kf, E2], writes=[BK])
                S.op("act", lambda e: e.activation(out=VB[:, 0:n], in_=Vf[:, 0:n], func=AF.Copy), reads=[Vf], writes=[VB])
                HH = [(hh, slice(hh * 64, hh * 64 + 64), slice(hh * 64, hh * 64 + Lc)) for hh in range(2)]
                PR = [slice(0, 128)] if Lc == 64 else [slice(0, Lc), slice(64, 64 + Lc)]
                for c in range(nch):
                    pbt = psb16(2 + c % 2)
                    fns = []
                    for (hh, hs, hl) in HH:
                        for i, srcap in enumerate([BK[hs, c, 0, :], BK[hs, c, 1, :], VB[hs, c * Lc:(c + 1) * Lc]]):
                            fns.append(lambda e, i=i, srcap=srcap, hs=hs, hl=hl: e.transpose(pbt[hl, i * 64:(i + 1) * 64], srcap, IDB[hs, hs]))
                    S.group("pe", fns, reads=[BK, VB, IDB], writes=[pbt])
                    for pr in PR:
                        S.op("act", lambda e, c=c, pbt=pbt, pr=pr: e.activation(out=TM[pr, c, :, :], in_=pbt[pr, 0:192].rearrange("p (a b) -> p a b", a=3), func=AF.Copy),
                             reads=[pbt], writes=[TM])
                for c in range(nch):
                    pbm = PB[4 + c % 2]
                    fns = []
                    for (hh, hs, hl) in HH:
                        fns.append(lambda e, c=c, hs=hs, hl=hl: e.matmul(pbm[hl, 0:2 * Lc], lhsT=BK[hs, c, 0, :], rhs=AR[hs, c, :, :], start=True, stop=True))
                        fns.append(lambda e, c=c, hs=hs, hl=hl: e.matmul(pbm[hl, 2 * Lc:4 * Lc], lhsT=BK[hs, c, 1, :], rhs=AR[hs, c, :, :], start=True, stop=True))
                        fns.append(lambda e, c=c, hs=hs, hl=hl: e.matmul(pbm[hl, 4 * Lc:5 * Lc], lhsT=AR[hs, c, 0, :], rhs=BK[hs, c, 0, :], start=True, stop=True))
                    S.group("pe", fns, reads=[AR, BK], writes=[pbm])
                    for pr in PR:
                        S.op("dve", lambda e, c=c, pbm=pbm, pr=pr: e.tensor_tensor(out=MTS[pr, c, :, :], in0=pbm[pr, 0:4 * Lc].rearrange("p (a b) -> p a b", a=4),
                                                                              in1=M4B[pr, :, 0:Lc], op=ALU.mult), reads=[pbm, M4B], writes=[MTS])
                        S.op("dve", lambda e, c=c, pbm=pbm, pr=pr: e.tensor_tensor(out=PT0[pr, c, :], in0=pbm[pr, 4 * Lc:5 * Lc], in1=MSB[pr, 0:Lc], op=ALU.mult),
                             reads=[pbm, MSB], writes=[PT0])
                for c0 in range(0, nch, 8):
                    c1 = min(nch, c0 + 8)
                    pbx = PB[6]
                    S.group("pe", [lambda e, c=c, hl=hl: e.matmul(pbx[hl, (c - c0) * 64:(c - c0 + 1) * 64], lhsT=MTS[hl, c, 2, :], rhs=TM[hl, c, 2, :], start=True, stop=True)
                                   for c in range(c0, c1) for (hh, hs, hl) in HH], reads=[MTS, TM], writes=[pbx])
                    for pr in PR:
                        S.op("act", lambda e, pr=pr: e.activation(out=X2S[pr, c0:c1, :], in_=pbx[pr, 0:(c1 - c0) * 64].rearrange("p (a b) -> p a b", b=64), func=AF.Copy),
                             reads=[pbx], writes=[X2S])
                for pr in PR:
                    S.op("dve", lambda e, pr=pr: e.tensor_tensor(out=WW[0][pr, :, :], in0=MTS[pr, :, 0, :], in1=IDR[pr, 0:nch, 0:Lc], op=ALU.add),
                         reads=[MTS, IDR], writes=[WW[0]])
                for lev in range(1, nlev + 1):
                    Pp = (lambda hl, b: MTS[hl, b, 0, :]) if lev == 1 else (lambda hl, b, q=PP[(lev - 1) % 2]: q[hl, b, :])
                    PTp = (lambda hl, b: PT0[hl, b, :]) if lev == 1 else (lambda hl, b, q=PPT[(lev - 1) % 2]: q[hl, b, :])
                    Pp_v = MTS if lev == 1 else PP[(lev - 1) % 2]
                    PTp_v = PT0 if lev == 1 else PPT[(lev - 1) % 2]
                    Pn, PTn = PP[lev % 2], PPT[lev % 2]
                    Wp, Wn = WW[(lev - 1) % 2], WW[lev % 2]
                    pa, pbk, pw = PB[4], PB[5], PB[6]
                    if lev < nlev:
                        S.group("pe", [lambda e, b=b, hl=hl: e.matmul(pa[hl, b * Lc:(b + 1) * Lc], lhsT=PTp(hl, b), rhs=Pp(hl, b), start=True, stop=True)
                                       for b in range(nch) for (hh, hs, hl) in HH], reads=[Pp_v, PTp_v], writes=[pa])
                        for pr in PR:
                            S.op("act", lambda e, pr=pr: e.activation(out=Pn[pr, :, :], in_=pa[pr, 0:nch * Lc].rearrange("p (a b) -> p a b", b=Lc), func=AF.Copy),
                                 reads=[pa], writes=[Pn])
                    S.group("pe", [lambda e, b=b, hl=hl: e.matmul(pbk[hl, b * Lc:(b + 1) * Lc], lhsT=Pp(hl, b), rhs=PTp(hl, b), start=True, stop=True)
                                   for b in range(nch) for (hh, hs, hl) in HH], reads=[Pp_v, PTp_v], writes=[pbk])
                    for pr in PR:
                        S.op("act", lambda e, pr=pr: e.activation(out=PTn[pr, :, :], in_=pbk[pr, 0:nch * Lc].rearrange("p (a b) -> p a b", b=Lc), func=AF.Copy),
                             reads=[pbk], writes=[PTn])
                    S.group("pe", [lambda e, b=b, hl=hl: e.matmul(pw[hl, b * Lc:(b + 1) * Lc], lhsT=PTn[hl, b, :], rhs=Wp[hl, b, :], start=True, stop=True)
                                   for b in range(nch) for (hh, hs, hl) in HH], reads=[PTn, Wp], writes=[pw])
                    for pr in PR:
                        S.op("dve", lambda e, pr=pr: e.tensor_tensor(out=Wn[pr, :, :], in0=pw[pr, 0:nch * Lc].rearrange("p (a b) -> p a b", b=Lc),
                                                                  in1=Wp[pr, :, :], op=ALU.add), reads=[pw, Wp], writes=[Wn])
                NT = WW[nlev % 2]
                if cfg.get("stop") == "pre" and l == cfg.get("stop_layer", 0):
                    raise StopBuild()
                ypb = PB[3]
                for c in range(nch):
                    px, pu, psn = PB[4 + c % 2], PB[6], PB[7]
                    S.group("pe", [lambda e, hs=hs, hl=hl, c=c: e.matmul(px[hl, 0:64], lhsT=AR[hs, c, 0, :], rhs=STB[hs, hp, :], start=True, stop=True)
                                   for (hh, hs, hl) in HH], reads=[AR, STB], writes=[px])
                    for pr in PR:
                        S.op("dve", lambda e, px=px, c=c, pr=pr: e.tensor_tensor(out=XS[pr, :], in0=px[pr, 0:64], in1=X2S[pr, c, :], op=ALU.add), reads=[px, X2S], writes=[XS])
                    S.group("pe", [lambda e, hl=hl, c=c: e.matmul(pu[hl, 0:64], lhsT=NT[hl, c, :], rhs=XS[hl, :], start=True, stop=True)
                                   for (hh, hs, hl) in HH], reads=[NT, XS], writes=[pu])
                    for pr in PR:
                        S.op("act", lambda e, pr=pr: e.activation(out=US[pr, :], in_=pu[pr, 0:64], func=AF.Copy), reads=[pu], writes=[US])
                    fns = []
                    for (hh, hs, hl) in HH:
                        yo = lambda hs=hs, c=c: ypb[hs, c * Lc:(c + 1) * Lc]
                        fns.append(lambda e, hs=hs, c=c, yo=yo: e.matmul(yo(), lhsT=STB[hs, hp, :], rhs=AR[hs, c, 1, :], start=True, stop=False))
                        fns.append(lambda e, hl=hl, c=c, yo=yo: e.matmul(yo(), lhsT=US[hl, :], rhs=MTS[hl, c, 1, :], start=False, stop=False))
                        fns.append(lambda e, hl=hl, c=c, yo=yo: e.matmul(yo(), lhsT=TM[hl, c, 2, :], rhs=MTS[hl, c, 3, :], start=False, stop=True))
                    S.group("pe", fns, reads=[STB, AR, US, MTS, TM], writes=[ypb])
                    fns = []
                    for (hh, hs, hl) in HH:
                        fns.append(lambda e, hs=hs, hl=hl, c=c: e.matmul(psn[hs, 0:64], lhsT=TM[hl, c, 0, :], rhs=US[hl, :], start=True, stop=False))
                        fns.append(lambda e, hs=hs, hl=hl, c=c: e.matmul(psn[hs, 0:64], lhsT=TM[hl, c, 1, :], rhs=TM[hl, c, 2, :], start=False, stop=True))
                    S.group("pe", fns, reads=[TM, US], writes=[psn])
                    stv = sub(ST, None, hp * 256, (hp + 1) * 256)
                    stbv = sub(STB, None, hp * 128, (hp + 1) * 128)
                    S.op("dve", lambda e: e.tensor_tensor(out=ST[:, hp, :], in0=psn[:, 0:64], in1=ST[:, hp, :], op=ALU.add), reads=[psn, stv], writes=[stv])
                    gcol = E1[:, c * Lc + Lc - 1:c * Lc + Lc]
                    S.op("dve", lambda e, gcol=gcol: e.tensor_scalar(out=ST[:, hp, :], in0=ST[:, hp, :], scalar1=gcol, scalar2=None, op0=ALU.mult),
                         reads=[stv, E1], writes=[stv])
                    S.op("act", lambda e: e.activation(out=STB[:, hp, :], in_=ST[:, hp, :], func=AF.Copy), reads=[stv], writes=[stbv])
                if cfg.get("stop") == "seq" and l == cfg.get("stop_layer", 0):
                    raise StopBuild()
                S.op("act", lambda e: e.activation(out=Yf[:, 0:n], in_=ypb[:, 0:n], func=AF.Copy), reads=[ypb], writes=[Yf])
                S.op("dve", lambda e: e.tensor_copy(out=T3B[:, 0:n], in_=Yf[:, 0:n]), reads=[Yf], writes=[T3B])
                S.op("pe", lambda e: e.matmul(PB[0][:, 0:n], lhsT=BLB[:, :], rhs=T3B[:, 0:n], start=True, stop=True), reads=[BLB, T3B], writes=[PB[0]])
                S.op("dve", lambda e: e.scalar_tensor_tensor(out=Yf[:, 0:n], in0=PB[0][:, 0:n], scalar=-1.0 / 64, in1=Yf[:, 0:n], op0=ALU.mult, op1=ALU.add),
                     reads=[PB[0], Yf], writes=[Yf])
                S.op("act", lambda e: e.activation(out=T3B[:, 0:n], in_=Yf[:, 0:n], func=AF.Square), reads=[Yf], writes=[T3B])
                S.op("pe", lambda e: e.matmul(PB[1][:, 0:n], lhsT=BLB[:, :], rhs=T3B[:, 0:n], start=True, stop=True), reads=[BLB, T3B], writes=[PB[1]])
                S.op("act", lambda e: e.activation(out=T1[:, 0:n], in_=PB[1][:, 0:n], func=AF.Sqrt, bias=EPS_G, scale=1.0 / 64), reads=[PB[1], CON], writes=[T1])
                S.op("dve", lambda e: e.reciprocal(out=T1[:, 0:n], in_=T1[:, 0:n]), reads=[T1], writes=[T1])
                S.op("dve", lambda e: e.tensor_tensor(out=Yf[:, 0:n], in0=Yf[:, 0:n], in1=T1[:, 0:n], op=ALU.mult), reads=[Yf, T1], writes=[Yf])
                S.op("dve", lambda e, hp=hp: e.tensor_scalar(out=Yf[:, 0:n], in0=Yf[:, 0:n], scalar1=vc(4, hp), scalar2=vc(5, hp), op0=ALU.mult, op1=ALU.add),
                     reads=[Yf, VEC], writes=[Yf])
                S.op("dve", lambda e: e.tensor_tensor(out=T2[:, 0:n], in0=Cc[:, 0:n], in1=Vf[:, 0:n], op=ALU.mult), reads=[Cc, Vf], writes=[T2])
                S.op("dve", lambda e: e.tensor_tensor(out=Yf[:, 0:n], in0=Yf[:, 0:n], in1=T2[:, 0:n], op=ALU.add), reads=[Yf, T2], writes=[Yf])
                ygv = sub(YG, None, hp * NA * 2, (hp + 1) * NA * 2)
                S.op("dve", lambda e, hp=hp: e.tensor_tensor(out=YG[:, hp, 0:n], in0=Yf[:, 0:n], in1=Gf[:, 0:n], op=ALU.mult), reads=[Yf, Gf], writes=[ygv])
                if cfg.get("stop") == "hp0" and l == cfg.get("stop_layer", 0):
                    raise StopBuild()
            if last:
                S.dma("sp", wkvO[l, 1 if sample else 0], ST[:, :, :], reads=[ST], slot="wkvo")
            if cfg.get("stop") == "p2" and l == cfg.get("stop_layer", 0):
                raise StopBuild()

            sb.pos = region_pos
            HO = sb.alloc(F32, [128, KT, NA])
            ffn_pos = sb.pos
            WO = [WCH[0][0], WCH[1][0]]
            for oc in range(KT):
                wo = WO[oc % 2]
                wload(wo, wo_d[l, oc], ("wch", oc % 2))
                pb = PB[oc % 2]
                proj_fm(pb, wo, YG, n)
                S.op("act", lambda e, oc=oc, pb=pb: e.activation(out=HO[:, oc, 0:n], in_=pb[:, 0:n], func=AF.Copy), reads=[pb],
                     writes=[sub(HO, None, oc * NA * 4, (oc + 1) * NA * 4)])
            load_x(XT, l, t0, n)
            post_and_ffn(l, XT, HO, n, t0, ffn_pos)

        def lam_setup(j):
            sb_mark = sb.pos
            sb.pos = base_pos
            LR = sb.alloc(F32, [128, 512])
            T = sb.alloc(F32, [128, 128])
            A2 = sb.alloc(F32, [128, 4])
            S.dma("sp", LR[:, :], lam_d[:, j, :], writes=[LR], slot="lam")
            for i in range(2):
                S.op("dve", lambda e, i=i: e.tensor_tensor(out=T[:, :], in0=LR[:, (2 * i) * 128:(2 * i + 1) * 128], in1=LR[:, (2 * i + 1) * 128:(2 * i + 2) * 128], op=ALU.mult),
                     reads=[LR], writes=[T])
                S.op("dve", lambda e, i=i: e.reduce_sum(out=A2[:, i:i + 1], in_=T[:, :], axis=AX.X), reads=[T], writes=[A2])
            S.op("act", lambda e: e.activation(out=A2[:, 0:2], in_=A2[:, 0:2], func=AF.Exp), reads=[A2], writes=[A2])
            S.op("dve", lambda e: e.tensor_tensor(out=A2[:, 2:3], in0=A2[:, 0:1], in1=A2[:, 1:2], op=ALU.subtract), reads=[A2], writes=[A2])
            S.op("dve", lambda e: e.tensor_scalar(out=LAMV[:, 4 * j:4 * j + 1], in0=A2[:, 2:3], scalar1=lambda_init(2 + j), scalar2=None, op0=ALU.add),
                 reads=[A2], writes=[LAMV])
            S.op("dve", lambda e: e.tensor_scalar(out=LAMV[:, 4 * j + 1:4 * j + 2], in0=LAMV[:, 4 * j:4 * j + 1], scalar1=-1.0, scalar2=None, op0=ALU.mult),
                 reads=[LAMV], writes=[LAMV])
            sb.pos = sb_mark

        def attn_tile(l, t0, n, sample):
            j = l - 2
            NA = NS if sample else TT
            cur["NA"] = NA
            sb.pos = base_pos
            XT = sb.alloc(F32, [128, KT, NA])
            HO = sb.alloc(F32, [128, KT, NA])
            ffn_pos = sb.pos
            XNB = sb.alloc(BF16, [128, KT, NA])
            QT = sb.alloc(BF16, [128, KT, NA])
            ATT = sb.alloc(BF16, [128, KT, NA])
            WQ = [sb.alloc(BF16, [128, KT, 128]) for _ in range(2)]
            QF = sb.alloc(F32, [128, NA])
            TB = sb.alloc(BF16, [128, NA])
            a_pos = sb.pos
            load_x(XT, l, t0, n)
            load_rope(t0, n)
            rms_stats(XT, n, PB[7], 1.0 / D, EPS_N)
            rms_apply(XNB, XT, VEC_NM + l * 2 + 0, n)
            for hm in range(KT):
                wq = WQ[hm % 2]
                wload(wq, aq_d[j, hm], ("wq", hm % 2))
                pb = PB[hm % 2]
                proj_fm(pb, wq, XNB, n)
                rope_evac(pb, n, t0, QF, TB, PB[2 + hm % 2])
                S.op("act", lambda e, hm=hm: e.activation(out=QT[:, hm, 0:n], in_=QF[:, 0:n], func=AF.Copy), reads=[QF],
                     writes=[sub(QT, None, hm * NA * 2, (hm + 1) * NA * 2)])
            scale = 128 ** -0.5
            if not sample:
                prompt_attention(j, t0, n, QT, ATT, a_pos, scale)
            else:
                sample_attention(j, t0, n, QT, ATT, a_pos, scale)
            sb.pos = a_pos
            WO = [sb.alloc(BF16, [128, KT, 128]) for _ in range(2)]
            for oc in range(KT):
                wo = WO[oc % 2]
                wload(wo, ao_d[j, oc], ("wao", oc % 2))
                pb = PB[oc % 2]
                proj_fm(pb, wo, ATT, n)
                S.op("act", lambda e, oc=oc, pb=pb: e.activation(out=HO[:, oc, 0:n], in_=pb[:, 0:n], func=AF.Copy), reads=[pb],
                     writes=[sub(HO, None, oc * NA * 4, (oc + 1) * NA * 4)])
            post_and_ffn(l, XT, HO, n, t0, ffn_pos)

        def post_and_ffn_attn(l, XT, HO, n, t0):
            post_and_ffn(l, XT, HO, n, t0, sb_after(HO))

        def sb_after(v):
            return v.hi

        def finish_head(j, ops_v, RS, qrows, ATT_write):
            pass

        def prompt_attention(j, t0, n, QT, ATT, a_pos, scale):
            sb.pos = a_pos
            KTH = [sb.alloc(BF16, [128, 2, NP]) for _ in range(2)]
            VH = [sb.alloc(BF16, [128, 16, 256]) for _ in range(2)]
            P = sb.alloc(BF16, [128, NP])
            PT = sb.alloc(BF16, [128, 16, 128])
            MX = sb.alloc(F32, [128, 8])
            O0 = sb.alloc(F32, [128, 256])
            DF = sb.alloc(F32, [128, 256])
            JK = sb.alloc(F32, [128, 256])
            ON = sb.alloc(BF16, [128, 256])
            kmax = t0 + n
            nkt_all = kmax // 128
            SC = psb(0, 4)
            for h in range(8):
                kth, vh = KTH[h % 2], VH[h % 2]
                for m in range(2):
                    S.dma("sp", kth[:, m, 0:kmax], KTs[2 * h + m, :, 0:kmax], reads=[dres("KTs")], writes=[kth], slot=("kth", h % 2))
                S.dma("sp", vh[:, 0:nkt_all, :], Vs[0:kmax, h * 256:(h + 1) * 256].rearrange("(k p) e -> p k e", p=128), reads=[dres("Vs")], writes=[vh],
                      slot=("vh", h % 2))
                for qi in range(n // 128):
                    kend = t0 + (qi + 1) * 128
                    nkt = kend // 128
                    OPS = PB[6]
                    for m in range(2):
                        hm = 2 * h + m
                        nkb = (kend + 511) // 512
                        S.group("pe", [lambda e, kb=kb, m=m, hm=hm: e.matmul(SC[:, kb * 512:min(kend, (kb + 1) * 512)], lhsT=QT[:, hm, qi * 128:(qi + 1) * 128],
                                                                          rhs=kth[:, m, kb * 512:min(kend, (kb + 1) * 512)], start=True, stop=True)
                                       for kb in range(nkb)], reads=[QT, kth], writes=[SC])
                        S.op("dve", lambda e: e.tensor_tensor(out=SC[:, kend - 128:kend], in0=SC[:, kend - 128:kend], in1=cc("cmask"), op=ALU.add),
                             reads=[SC, CON], writes=[SC])
                        S.op("dve", lambda e: e.reduce_max(out=MX[:, 0:1], in_=SC[:, 0:kend], axis=AX.X), reads=[SC], writes=[MX])
                        S.op("dve", lambda e: e.tensor_scalar(out=MX[:, 1:2], in0=MX[:, 0:1], scalar1=-scale, scalar2=None, op0=ALU.mult), reads=[MX], writes=[MX])
                        S.op("act", lambda e, m=m: e.activation(out=P[:, 0:kend], in_=SC[:, 0:kend], func=AF.Exp, bias=MX[:, 1:2], scale=scale,
                                                               accum_out=MX[:, 2 + m:3 + m]), reads=[SC, MX], writes=[P, MX])
                        for g0 in range(0, nkt, 8):
                            g1 = min(nkt, g0 + 8)
                            pbt = psb16(4 + (g0 // 8) % 2)
                            S.group("pe", [lambda e, kt=kt, pbt=pbt: e.transpose(pbt[:, (kt - g0) * 128:(kt - g0 + 1) * 128], P[:, kt * 128:(kt + 1) * 128], IDB[:, :])
                                           for kt in range(g0, g1)], reads=[P, IDB], writes=[pbt])
                            S.op("act", lambda e, pbt=pbt: e.activation(out=PT[:, g0:g1, :], in_=pbt[:, 0:(g1 - g0) * 128].rearrange("p (a b) -> p a b", b=128), func=AF.Copy),
                                 reads=[pbt], writes=[PT])
                        S.group("pe", [lambda e, kt=kt, m=m: e.matmul(OPS[:, m * 256:(m + 1) * 256], lhsT=PT[:, kt, :], rhs=vh[:, kt, :], start=(kt == 0), stop=(kt == nkt - 1))
                                       for kt in range(nkt)], reads=[PT, vh], writes=[OPS])
                    head_finish(j, OPS, MX, 128, O0, DF, JK, ON,
                                lambda ec, pbt: S.op("act", lambda e: e.activation(out=ATT[:, 2 * h + ec, qi * 128:(qi + 1) * 128], in_=pbt[:, ec * 128:(ec + 1) * 128], func=AF.Copy),
                                                     reads=[pbt], writes=[sub(ATT, None, (2 * h + ec) * TT * 2, (2 * h + ec + 1) * TT * 2)]))

        def head_finish(j, OPS, MX, rows, O0, DF, JK, ON, write_fn):
            R = slice(0, rows)
            S.op("dve", lambda e: e.reciprocal(out=MX[R, 4:6], in_=MX[R, 2:4]), reads=[MX], writes=[MX])
            S.op("dve", lambda e: e.tensor_tensor(out=MX[R, 6:7], in0=MX[R, 5:6], in1=LAMV[R, 4 * j + 1:4 * j + 2], op=ALU.mult), reads=[MX, LAMV], writes=[MX])
            S.op("dve", lambda e: e.tensor_scalar(out=O0[R, :], in0=OPS[R, 0:256], scalar1=MX[R, 4:5], scalar2=None, op0=ALU.mult), reads=[OPS, MX], writes=[O0])
            S.op("dve", lambda e: e.scalar_tensor_tensor(out=DF[R, :], in0=OPS[R, 256:512], scalar=MX[R, 6:7], in1=O0[R, :], op0=ALU.mult, op1=ALU.add),
                 reads=[OPS, MX, O0], writes=[DF])
            sublayer_norm(j, DF, rows, MX, JK, ON, write_fn)

        def sublayer_norm(j, DF, rows, MX, JK, ON, write_fn):
            R = slice(0, rows)
            S.op("act", lambda e: e.activation(out=JK[R, :], in_=DF[R, :], func=AF.Square, accum_out=MX[R, 7:8]), reads=[DF], writes=[JK, MX])
            S.op("act", lambda e: e.activation(out=MX[R, 7:8], in_=MX[R, 7:8], func=AF.Sqrt, bias=EPS_S[R, :], scale=1.0 / 256), reads=[MX, CON], writes=[MX])
            S.op("dve", lambda e: e.reciprocal(out=MX[R, 7:8], in_=MX[R, 7:8]), reads=[MX], writes=[MX])
            S.op("dve", lambda e: e.tensor_scalar(out=MX[R, 7:8], in0=MX[R, 7:8], scalar1=1.0 - lambda_init(2 + j), scalar2=None, op0=ALU.mult), reads=[MX], writes=[MX])
            S.op("dve", lambda e: e.scalar_tensor_tensor(out=ON[R, :], in0=DF[R, :], scalar=MX[R, 7:8], in1=SUBW[R, j, :], op0=ALU.mult, op1=ALU.mult),
                 reads=[DF, MX, SUBW], writes=[ON])
            pbt = psb16(7)
            S.group("pe", [lambda e, ec=ec: e.transpose(pbt[:, ec * 128:ec * 128 + rows], ON[R, ec * 128:(ec + 1) * 128], IDB[R, R]) for ec in range(2)],
                    reads=[ON, IDB], writes=[pbt])
            for ec in range(2):
                write_fn(ec, pbt)

        def sample_attention(j, t0, n, QT, ATT, a_pos, scale):
            sb.pos = a_pos
            QB = sb.alloc(BF16, [128, 16, 128])
            IDXF = sb.alloc(F32, [128, NPAGE])
            IDX = sb.alloc(I32, [128, NPAGE])
            PTI = sb.alloc(I32, [128, NPAGE])
            KP = [sb.alloc(BF16, [128, D]) for _ in range(2)]
            KTP = [sb.alloc(BF16, [128, 16, 128]) for _ in range(2)]
            NKEY = NPAGE * PAGE + NS
            SCS = sb.alloc(F32, [128, NKEY])
            PS_ = sb.alloc(BF16, [128, NKEY])
            PTB = [sb.alloc(BF16, [128, 128]) for _ in range(2)]
            KN = sb.alloc(BF16, [128, 16, NS])
            VN = sb.alloc(BF16, [NS, D])
            MX = sb.alloc(F32, [128, 8])
            OS = sb.alloc(F32, [128, 8, 256])
            OSEL = sb.alloc(F32, [128, 256])
            DM = sb.alloc(F32, [128, 64])
            O0 = sb.alloc(F32, [128, 256]); DF = sb.alloc(F32, [128, 256]); JK = sb.alloc(F32, [128, 256]); ON = sb.alloc(BF16, [128, 256])
            S.op("pool", lambda e: e.memset(QB[:, :, :], 0.0), writes=[QB])
            for hm in range(16):
                h, m = hm // 2, hm % 2
                c0 = m * 64 + h * 8
                S.op("dve", lambda e, hm=hm, c0=c0: e.tensor_copy(out=QB[:, hm, c0:c0 + NS], in_=QT[:, hm, 0:NS]), reads=[QT], writes=[QB])
            S.dma("sp", PTI[:, :], ptab.partition_broadcast(128), writes=[PTI], slot="pti")
            S.op("dve", lambda e: e.tensor_copy(out=IDXF[:, :], in_=PTI[:, :]), reads=[PTI], writes=[IDXF])
            S.op("dve", lambda e: e.tensor_scalar(out=IDXF[:, :], in0=IDXF[:, :], scalar1=128.0, scalar2=cc("pidx")[:, 0:1], op0=ALU.mult, op1=ALU.add),
                 reads=[IDXF, CON], writes=[IDXF])
            S.op("dve", lambda e: e.tensor_copy(out=IDX[:, :], in_=IDXF[:, :]), reads=[IDXF], writes=[IDX])

            def gather(dst, src_d, pg, slot):
                S.dma("pool", None, None, reads=[IDX], writes=[dst], slot=slot,
                      fn=lambda e: e.indirect_dma_start(out=dst[:, :], out_offset=None, in_=src_d,
                                                        in_offset=bass.IndirectOffsetOnAxis(ap=IDX[:, pg:pg + 1].bitcast(U32), axis=0)))

            for pg in range(NPAGE):
                kp, ktp = KP[pg % 2], KTP[pg % 2]
                gather(kp, cache_k, pg, ("kp", pg % 2))
                for half in range(2):
                    pbt = psb16(half)
                    S.group("pe", [lambda e, hm=hm, pbt=pbt: e.transpose(pbt[:, (hm % 8) * 128:(hm % 8 + 1) * 128], kp[:, hm * 128:(hm + 1) * 128], IDB[:, :])
                                   for hm in range(half * 8, half * 8 + 8)], reads=[kp, IDB], writes=[pbt])
                    S.op("act" if half == 0 else "dve",
                         (lambda e, pbt=pbt, half=half: e.activation(out=ktp[:, half * 8:half * 8 + 8, :], in_=pbt[:, :].rearrange("p (a b) -> p a b", b=128), func=AF.Copy)) if half == 0 else
                         (lambda e, pbt=pbt, half=half: e.tensor_copy(out=ktp[:, half * 8:half * 8 + 8, :], in_=pbt[:, :].rearrange("p (a b) -> p a b", b=128))),
                         reads=[pbt], writes=[ktp])
                pbs = PB[2 + (pg // 4) % 2]
                S.group("pe", [lambda e, hm=hm: e.matmul(pbs[:, (pg % 4) * 128:(pg % 4 + 1) * 128], lhsT=QB[:, hm, :], rhs=ktp[:, hm, :], start=(hm == 0), stop=(hm == 15))
                               for hm in range(16)], reads=[QB, ktp], writes=[pbs])
                if pg % 4 == 3:
                    g = pg // 4
                    S.op("act", lambda e, g=g, pbs=pbs: e.activation(out=SCS[:, g * 512:(g + 1) * 512], in_=pbs[:, :], func=AF.Copy), reads=[pbs],
                         writes=[sub(SCS, None, g * 2048, (g + 1) * 2048)])
            S.dma("sp", KN[:, :, :], KTs[:, :, NP:NP + NS].rearrange("k p t -> p k t"), reads=[dres("KTs")], writes=[KN], slot="kn")
            S.dma("sp", VN[:, :], Vs[NP:NP + NS, :], reads=[dres("Vs")], writes=[VN], slot="vn")
            pbs = PB[4]
            S.group("pe", [lambda e, hm=hm: e.matmul(pbs[:, 0:NS], lhsT=QB[:, hm, :], rhs=KN[:, hm, :], start=(hm == 0), stop=(hm == 15)) for hm in range(16)],
                    reads=[QB, KN], writes=[pbs])
            tail = sub(SCS, None, NPAGE * PAGE * 4, NKEY * 4)
            S.op("dve", lambda e: e.tensor_tensor(out=SCS[:, NPAGE * PAGE:NKEY], in0=pbs[:, 0:NS], in1=cc("smask"), op=ALU.add), reads=[pbs, CON], writes=[tail])
            S.op("dve", lambda e: e.reduce_max(out=MX[:, 0:1], in_=SCS[:, :], axis=AX.X), reads=[SCS], writes=[MX])
            S.op("dve", lambda e: e.tensor_scalar(out=MX[:, 1:2], in0=MX[:, 0:1], scalar1=-scale, scalar2=None, op0=ALU.mult), reads=[MX], writes=[MX])
            S.op("act", lambda e: e.activation(out=PS_[:, :], in_=SCS[:, :], func=AF.Exp, bias=MX[:, 1:2], scale=scale, accum_out=MX[:, 2:3]),
                 reads=[SCS, MX], writes=[PS_, MX])
            OA = psb(4, 4)
            nblk = NPAGE + 1
            for pg in range(nblk):
                lastb = pg == NPAGE
                kk_ = NS if lastb else 128
                ptb = PTB[pg % 2]
                pbt = psb16(pg % 2)
                S.op("pe", lambda e, pg=pg, kk_=kk_, pbt=pbt: e.transpose(pbt[0:kk_, 0:128], PS_[:, pg * 128:pg * 128 + kk_], IDB[:, :]), reads=[PS_, IDB], writes=[pbt])
                S.op("act", lambda e, kk_=kk_, pbt=pbt, ptb=ptb: e.activation(out=ptb[0:kk_, :], in_=pbt[0:kk_, 0:128], func=AF.Copy), reads=[pbt], writes=[ptb])
                if lastb:
                    vsrc = VN
                    vap = lambda f: VN[0:NS, f * 512:(f + 1) * 512]
                else:
                    vsrc = KP[pg % 2]
                    gather(vsrc, cache_v, pg, ("kp", pg % 2))
                    vap = lambda f, vsrc=vsrc: vsrc[:, f * 512:(f + 1) * 512]
                S.group("pe", [lambda e, f=f, vap=vap, ptb=ptb, kk_=kk_: e.matmul(OA[:, f * 512:(f + 1) * 512], lhsT=ptb[0:kk_, :], rhs=vap(f), start=(pg == 0), stop=lastb,
                                                                           skip_group_check=True)
                               for f in range(4)], reads=[ptb, vsrc], writes=[OA])
            S.op("dve", lambda e: e.tensor_tensor(out=OS[:, :, :], in0=OA[:, :].rearrange("p (h e) -> p h e", h=8),
                                                  in1=cc("selh").unsqueeze(2).to_broadcast([128, 8, 256]), op=ALU.mult), reads=[OA, CON], writes=[OS])
            S.op("dve", lambda e: e.tensor_reduce(out=OSEL[:, :], in_=OS[:, :, :].rearrange("p h e -> p e h"), op=ALU.add, axis=AX.X), reads=[OS], writes=[OSEL])
            S.op("dve", lambda e: e.reciprocal(out=MX[:, 3:4], in_=MX[:, 2:3]), reads=[MX], writes=[MX])
            S.op("dve", lambda e: e.tensor_scalar(out=OSEL[:, :], in0=OSEL[:, :], scalar1=MX[:, 3:4], scalar2=None, op0=ALU.mult), reads=[OSEL, MX], writes=[OSEL])
            S.op("dve", lambda e: e.tensor_copy(out=DM[0:64, :], in_=cc("dsel", 64)), reads=[CON], writes=[DM])
            S.op("dve", lambda e: e.tensor_scalar(out=DM[64:128, :], in0=cc("dsel")[64:128, :], scalar1=LAMV[64:128, 4 * j + 1:4 * j + 2], scalar2=None, op0=ALU.mult),
                 reads=[CON, LAMV], writes=[DM])
            pd = PB[0]
            S.op("pe", lambda e: e.matmul(pd[0:64, 0:256], lhsT=DM[:, :], rhs=OSEL[:, :], start=True, stop=True), reads=[DM, OSEL], writes=[pd])
            S.op("act", lambda e: e.activation(out=DF[0:64, :], in_=pd[0:64, 0:256], func=AF.Copy), reads=[pd], writes=[DF])

            def wr(ec, pbt):
                for h in range(8):
                    S.op("act", lambda e, h=h: e.activation(out=ATT[:, 2 * h + ec, 0:NS], in_=pbt[:, ec * 128 + h * 8:ec * 128 + h * 8 + 8], func=AF.Copy),
                         reads=[pbt], writes=[ATT])
            sublayer_norm(j, DF, 64, MX, JK, ON, wr)

        tiles = [(i * TT, TT, i == 0, i == NP // TT - 1, False) for i in range(NP // TT)] + [(NP, NS, True, True, True)]
        tiles = tiles[:cfg["tiles"]] if cfg["tiles"] < 5 else tiles
        if cfg.get("only_sample"):
            tiles = [tiles[-1]]
        try:
            for l in range(cfg["layers"]):
                if l >= 2:
                    lam_setup(l - 2)
                for (t0, n, first, last, sample) in tiles:
                    if l < 2:
                        rwkv_tile(l, t0, n, first, last, sample)
                    else:
                        attn_tile(l, t0, n, sample)
        except StopBuild:
            pass
        S.finish("sp")
        print("instructions emitted:", S.ninst, "dma sems:", len(S.dsem), "sbuf peak?", sb.pos, "cnt", S.cnt, "maxdma", max(v[1] for v in S.dsem.values()))
    return nc


def prepare_shared(inp):
    g = lambda k: np.asarray(inp[k])
    sh = {}
    sh["consts"] = make_consts()
    vecs = np.zeros((128, NV, KT), np.float32)
    for l in range(4):
        for i in range(2):
            vecs[:, VEC_NM + l * 2 + i] = vec_layout(g("norm_mix")[l, i])
            vecs[:, VEC_NF + l * 2 + i] = vec_layout(g("norm_ffn")[l, i])
    for l in range(2):
        for i in range(6):
            vecs[:, VEC_MU + l * 6 + i] = vec_layout(g("rwkv_mu")[l, i])
            vecs[:, VEC_VC + l * 6 + i] = vec_layout(g("rwkv_vec")[l, i])
        vecs[:, VEC_RK + l] = vec_layout(g("rwkv_rk")[l].reshape(-1))
    vecs[:, VEC_V0] = vec_layout(g("rwkv_v0")[0])
    vecs[:, VEC_KVN] = vec_layout(g("kv_norm"))
    sh["vecs"] = vecs
    cos, sin = rope_tables()
    sh["cos"], sh["sin"] = cos, sin
    sh["lamrep"] = np.ascontiguousarray(np.broadcast_to(g("attn_lambda").reshape(1, 2, 512), (128, 2, 512))).astype(np.float32)
    subw = np.stack([np.broadcast_to(g("attn_subln")[j][None, :], (128, 256)) for j in range(2)], axis=1).astype(np.float32)
    sh["subw"] = np.ascontiguousarray(subw)
    for nm, key in [("wr", "rwkv_wr"), ("wk", "rwkv_wk"), ("wv", "rwkv_wv"), ("wo", "rwkv_wo"), ("aq", "attn_wq"), ("ao", "attn_wo")]:
        sh[nm] = np.stack([w_cols(g(key)[l]) for l in range(2)])
    sh["lw1"] = np.stack([np.ascontiguousarray(g("rwkv_w1")[l].reshape(KT, 128, 96).transpose(1, 0, 2)) for l in range(2)])
    sh["la1"] = np.stack([np.ascontiguousarray(g("rwkv_a1")[l].reshape(KT, 128, 96).transpose(1, 0, 2)) for l in range(2)])
    sh["lv1"] = np.ascontiguousarray(g("rwkv_v1")[0].reshape(KT, 128, 64).transpose(1, 0, 2))[None]
    sh["lg1"] = np.stack([np.ascontiguousarray(g("rwkv_g1")[l].reshape(KT, 128, 256).transpose(1, 0, 2)) for l in range(2)])
    sh["lw2"] = np.ascontiguousarray(g("rwkv_w2"))
    sh["la2"] = np.ascontiguousarray(g("rwkv_a2"))
    sh["lv2"] = np.ascontiguousarray(g("rwkv_v2"))
    sh["lg2"] = np.stack([np.ascontiguousarray(g("rwkv_g2")[l].reshape(2, 128, D).transpose(1, 0, 2)) for l in range(2)])
    sh["kvk"] = w_cols(g("kv_wk"))
    sh["kvv"] = w_cols(g("kv_wv"), 512)
    sh["f1"] = np.stack([w_cols(g("ffn_w1")[l]) for l in range(4)])
    w2 = g("ffn_w2")
    sh["f2"] = np.stack([np.ascontiguousarray(w2[l].reshape(64, 128, KT, 128).transpose(2, 1, 0, 3)) for l in range(4)])
    sh["cache_k"] = g("cache_k").reshape(NPOOL * PAGE, D)
    sh["cache_v"] = g("cache_v").reshape(NPOOL * PAGE, D)
    return sh


def core_inputs(inp, sh, c):
    g = lambda k: np.asarray(inp[k])
    m = dict(sh)
    xs = np.concatenate([g("x_prompt")[c // 2], g("x_sample")[c]], axis=0)
    m["x0"] = np.ascontiguousarray(xs.T.reshape(KT, 128, NTOK))
    m["shift0"] = np.stack([vec_layout(g("state_shift")[l, c]) for l in range(2)])
    wk = g("state_wkv")[:, c]
    m["wkv0"] = np.ascontiguousarray(wk.reshape(2, KT, 2, 64, 64).transpose(0, 2, 4, 1, 3).reshape(2, 128, KT, 64))
    m["ptab"] = np.ascontiguousarray(g("page_table")[c].reshape(1, NPAGE).astype(np.int32))
    return m


def assemble(results):
    yp = np.zeros((4, NP, D), np.float32); ys = np.zeros((8, NS, D), np.float32)
    kp = np.zeros((4, NP, 8, 2, 128), np.float32); vp = np.zeros((4, NP, 8, 256), np.float32)
    ks = np.zeros((8, NS, 8, 2, 128), np.float32); vs = np.zeros((8, NS, 8, 256), np.float32)
    shp = np.zeros((2, 4, D), np.float32); shs = np.zeros((2, 8, D), np.float32)
    wp = np.zeros((2, 4, 32, 64, 64), np.float32); ws = np.zeros((2, 8, 32, 64, 64), np.float32)
    for c, r in enumerate(results):
        y = r["yT"].reshape(D, NTOK).T
        k = r["kT"].reshape(D, NTOK).T
        v = r["vO"]
        ys[c] = y[NP:]
        ks[c] = k[NP:].reshape(NS, 8, 2, 128)
        vs[c] = v[NP:].reshape(NS, 8, 256)
        sho = r["shO"]
        wo = r["wkvO"]
        wfix = lambda a: a.reshape(2, 64, KT, 64).transpose(2, 0, 3, 1).reshape(32, 64, 64)
        for l in range(2):
            shs[l, c] = sho[l, 1].T.reshape(D)
            ws[l, c] = wfix(wo[l, 1])
        if c % 2 == 0:
            s = c // 2
            yp[s] = y[:NP]
            kp[s] = k[:NP].reshape(NP, 8, 2, 128)
            vp[s] = v[:NP].reshape(NP, 8, 256)
            for l in range(2):
                shp[l, s] = sho[l, 0].T.reshape(D)
                wp[l, s] = wfix(wo[l, 0])
    return (yp, ys, kp, vp, shp, wp, ks, vs, shs, ws)


def kernel(**inputs):
    sh = prepare_shared(inputs)
    in_maps = [core_inputs(inputs, sh, c) for c in range(8)]
    nc = build_program(CFG)
    res = run_bass_kernel_spmd(nc, in_maps, core_ids=list(range(8)))
    return assemble(res.results)
```

```python
import contextlib
import math
import numpy as np
import concourse.bass as bass
import concourse.mybir as mybir
from concourse.bass_utils import run_bass_kernel_spmd

F32 = mybir.dt.float32
BF16 = mybir.dt.bfloat16
I32 = mybir.dt.int32
U32 = mybir.dt.uint32
AF = mybir.ActivationFunctionType
ALU = mybir.AluOpType
AX = mybir.AxisListType

D = 2048
KT = 16
NP = 2048
NS = 8
NTOK = NP + NS
TT = 512
LCH = 64
NPAGE = 128
PAGE = 128
NPOOL = 1280
GN_EPS = 64e-5
NORM_EPS = 1e-6
SUBLN_EPS = 1e-5
NEG = -30000.0
ESZ = {F32: 4, BF16: 2, I32: 4, U32: 4}

CFG = dict(layers=4, tiles=5, stop=None, npool=NPOOL)


class V:
    def __init__(self, ap, arena, lo, hi):
        self.ap, self.arena, self.lo, self.hi = ap, arena, lo, hi

    def __getitem__(self, k):
        return self.ap[k]


class Sched:
    def __init__(self, nc, es):
        self.nc = nc
        self.es = es
        self.eng = dict(pe=nc.tensor, act=nc.scalar, dve=nc.vector, pool=nc.gpsimd, sp=nc.sync)
        self.sem = {k: es.enter_context(nc.semaphore("s_" + k)) for k in self.eng}
        self.cnt = {k: 0 for k in self.eng}
        self.seen = {k: {} for k in self.eng}
        self.recs = {}
        self.dsem = {}
        self.ninst = 0

    def _deps(self, v, mode, out, e=None):
        lst = self.recs.get(v.arena)
        if not lst:
            return
        ps = v.arena == "ps"
        for (lo, hi, kind, tk) in lst:
            if lo < v.hi and v.lo < hi and (mode == "w" or kind == "w" or (ps and tk[0] != e)):
                out.add(tk)

    def _rec(self, v, mode, tk):
        lst = self.recs.setdefault(v.arena, [])
        if mode == "w":
            lst[:] = [r for r in lst if not (v.lo <= r[0] and r[1] <= v.hi)]
        else:
            lst[:] = [r for r in lst if not (r[2] == "r" and r[3][0] == tk[0] and v.lo <= r[0] and r[1] <= v.hi)]
        lst.append((v.lo, v.hi, mode, tk))

    def _wait(self, e, tk):
        key, val = tk
        if key in self.eng:
            if key == e == "pe":
                return
            sem = self.sem[key]
        else:
            sem, tot = self.dsem[key]
            val = tot
        if self.seen[e].get(key, 0) >= val:
            return
        self.eng[e].wait_ge(sem, val)
        self.seen[e][key] = val
        self.ninst += 1

    def op(self, e, fn, reads=(), writes=()):
        tks = set()
        for v in reads:
            self._deps(v, "r", tks, e)
        for v in writes:
            self._deps(v, "w", tks, e)
        for tk in sorted(tks, key=str):
            self._wait(e, tk)
        inst = fn(self.eng[e])
        self.cnt[e] += 1
        inst.then_inc(self.sem[e], 1)
        tk = (e, self.cnt[e])
        for v in reads:
            self._rec(v, "r", tk)
        for v in writes:
            self._rec(v, "w", tk)
        self.ninst += 1
        return inst

    def group(self, e, fns, reads=(), writes=()):
        tks = set()
        for v in reads:
            self._deps(v, "r", tks, e)
        for v in writes:
            self._deps(v, "w", tks, e)
        for tk in sorted(tks, key=str):
            self._wait(e, tk)
        inst = None
        for fn in fns:
            inst = fn(self.eng[e])
            self.ninst += 1
        self.cnt[e] += 1
        inst.then_inc(self.sem[e], 1)
        tk = (e, self.cnt[e])
        for v in reads:
            self._rec(v, "r", tk)
        for v in writes:
            self._rec(v, "w", tk)

    def _slot(self, key):
        if key not in self.dsem:
            s = self.es.enter_context(self.nc.semaphore("d%d" % len(self.dsem)))
            self.dsem[key] = [s, 0]
        return self.dsem[key]

    def dma(self, q, out_ap, in_ap, reads=(), writes=(), slot=None, fn=None):
        tks = set()
        for v in reads:
            self._deps(v, "r", tks, q)
        for v in writes:
            self._deps(v, "w", tks, q)
        for tk in sorted(tks, key=str):
            self._wait(q, tk)
        ds = self._slot(slot)
        if fn is None:
            inst = self.eng[q].dma_start(out=out_ap, in_=in_ap)
        else:
            inst = fn(self.eng[q])
        ds[1] += 16
        inst.then_inc(ds[0], 16)
        tk = (slot, ds[1])
        for v in reads:
            self._rec(v, "r", tk)
        for v in writes:
            self._rec(v, "w", tk)
        self.ninst += 1

    def finish(self, e="sp"):
        for key, (sem, tot) in self.dsem.items():
            if tot > 0 and self.seen[e].get(key, 0) < tot:
                self.eng[e].wait_ge(sem, tot)
        for k in self.eng:
            if k != e and self.cnt[k] > 0:
                self.eng[e].wait_ge(self.sem[k], self.cnt[k])


class StopBuild(Exception):
    pass


class Arena:
    def __init__(self, ap, name, nbytes):
        self.ap, self.name, self.nbytes, self.pos = ap, name, nbytes, 0

    def alloc(self, dtype, shape):
        esz = ESZ[dtype]
        n = int(np.prod(shape[1:]))
        nb = (n * esz + 63) // 64 * 64
        lo = self.pos
        self.pos += nb
        assert self.pos <= self.nbytes, (self.name, self.pos, self.nbytes)
        ap = self.ap[0:shape[0], lo // 4:(lo + nb) // 4]
        if dtype != F32:
            ap = ap.bitcast(dtype)
        ap = ap[:, 0:n]
        if len(shape) == 3:
            ap = ap.rearrange("p (a b) -> p a b", a=shape[1])
        elif len(shape) == 4:
            ap = ap.rearrange("p (a b c) -> p a b c", a=shape[1], b=shape[2])
        return V(ap, self.name, lo, lo + nb)


def sub(v, ap, frac_lo=None, frac_hi=None):
    lo = v.lo if frac_lo is None else v.lo + frac_lo
    hi = v.hi if frac_hi is None else v.lo + frac_hi
    return V(ap, v.arena, lo, hi)


VEC_NM, VEC_NF, VEC_MU, VEC_VC, VEC_V0, VEC_RK, VEC_KVN = 0, 8, 16, 28, 40, 41, 43
NV = 44

def _const_layout():
    lay = {}
    pos = 0
    for name, n in [("ident", 128), ("ones", 128), ("blk", 128), ("rot", 128), ("cmask", 128),
                    ("m4", 256), ("ms", 64), ("idrep", 1024), ("reset", 512), ("eps", 4),
                    ("smask", 8), ("selh", 8), ("dsel", 64), ("pidx", 1)]:
        lay[name] = (pos, n)
        pos += n
    return lay, pos


CL, NCONST = _const_layout()


def make_consts():
    c = np.zeros((128, NCONST), np.float32)

    def put(name, arr):
        o, n = CL[name]
        c[:arr.shape[0], o:o + n] = arr

    put("ident", np.eye(128, dtype=np.float32))
    put("ones", np.ones((128, 128), np.float32))
    blk = np.zeros((128, 128), np.float32)
    blk[:64, :64] = 1
    blk[64:, 64:] = 1
    put("blk", blk)
    rot = np.zeros((128, 128), np.float32)
    for m in range(64):
        rot[m + 64, m] = -1.0
        rot[m, m + 64] = 1.0
    put("rot", rot)
    qi = np.arange(128)[:, None]
    kj = np.arange(128)[None, :]
    put("cmask", np.where(kj <= qi, 0.0, NEG).astype(np.float32))
    j = np.arange(64)[:, None]
    t = np.arange(64)[None, :]
    strict = (j < t).astype(np.float32)
    incl = (j <= t).astype(np.float32)
    put("m4", np.tile(np.concatenate([strict, incl, strict, incl], axis=1), (2, 1)))
    put("ms", np.tile((t < j).astype(np.float32), (2, 1)))
    put("idrep", np.tile(np.eye(64, dtype=np.float32), (2, 16)))
    rs = np.ones((128, 512), np.float32)
    rs[:, ::64] = 0
    put("reset", rs)
    eps = np.zeros((128, 4), np.float32)
    eps[:, 0] = NORM_EPS
    eps[:, 1] = GN_EPS
    eps[:, 2] = SUBLN_EPS
    eps[:, 3] = 1e-12
    put("eps", eps)
    r = np.arange(128)
    q_of = r % 8
    h_of = (r % 64) // 8
    put("smask", np.where(np.arange(8)[None, :] <= q_of[:, None], 0.0, NEG).astype(np.float32))
    put("selh", (np.arange(8)[None, :] == h_of[:, None]).astype(np.float32))
    ds = np.zeros((128, 64), np.float32)
    ds[np.arange(128), np.arange(128) % 64] = 1.0
    put("dsel", ds)
    put("pidx", np.arange(128, dtype=np.float32)[:, None])
    return c


def vec_layout(v):
    return np.ascontiguousarray(np.asarray(v, np.float32).reshape(KT, 128).T)


def w_cols(w, cw=128):
    w = np.asarray(w, np.float32)
    din, dout = w.shape
    return np.ascontiguousarray(w.reshape(din // 128, 128, dout // cw, cw).transpose(2, 1, 0, 3))


def rope_tables():
    half = 64
    inv = np.power(np.float32(10000.0), -np.arange(half, dtype=np.float32) / np.float32(half)).astype(np.float32)
    pos = np.concatenate([np.arange(NP), 16384 + np.arange(NS)]).astype(np.float32)
    ang = (pos[None, :] * inv[:, None]).astype(np.float32)
    cos = np.cos(ang).astype(np.float32)
    sin = np.sin(ang).astype(np.float32)
    return np.ascontiguousarray(np.concatenate([cos, cos], 0)), np.ascontiguousarray(np.concatenate([sin, sin], 0))


def lambda_init(layer_idx):
    return 0.8 - 0.6 * math.exp(-0.3 * layer_idx)


def build_program(cfg=CFG):
    nc = bass.Bass("TRN2", target_bir_lowering=False)
    es = contextlib.ExitStack()

    def din(name, shape, dt=F32):
        return nc.dram_tensor(name, list(shape), dt, kind="ExternalInput").ap()

    def dout(name, shape, dt=F32):
        return nc.dram_tensor(name, list(shape), dt, kind="ExternalOutput").ap()

    def dint(name, shape, dt=F32):
        return nc.dram_tensor(name, list(shape), dt, kind="Internal").ap()

    x0 = din("x0", [KT, 128, NTOK])
    shift0 = din("shift0", [2, 128, KT])
    wkv0 = din("wkv0", [2, 128, KT, 64])
    ptab = din("ptab", [1, NPAGE], I32)
    cache_k = din("cache_k", [cfg["npool"] * PAGE, D])
    cache_v = din("cache_v", [cfg["npool"] * PAGE, D])
    consts_d = din("consts", [128, NCONST])
    vecs_d = din("vecs", [128, NV, KT])
    cos_d = din("cos", [128, NTOK])
    sin_d = din("sin", [128, NTOK])
    lam_d = din("lamrep", [128, 2, 512])
    subw_d = din("subw", [128, 2, 256])
    wr_d = din("wr", [2, KT, 128, KT, 128])
    wk_d = din("wk", [2, KT, 128, KT, 128])
    wv_d = din("wv", [2, KT, 128, KT, 128])
    wo_d = din("wo", [2, KT, 128, KT, 128])
    lw1_d = din("lw1", [2, 128, KT, 96])
    la1_d = din("la1", [2, 128, KT, 96])
    lv1_d = din("lv1", [1, 128, KT, 64])
    lg1_d = din("lg1", [2, 128, KT, 256])
    lw2_d = din("lw2", [2, 96, D])
    la2_d = din("la2", [2, 96, D])
    lv2_d = din("lv2", [1, 64, D])
    lg2_d = din("lg2", [2, 128, 2, D])
    kvk_d = din("kvk", [KT, 128, KT, 128])
    kvv_d = din("kvv", [4, 128, KT, 512])
    aq_d = din("aq", [2, KT, 128, KT, 128])
    ao_d = din("ao", [2, KT, 128, KT, 128])
    f1_d = din("f1", [4, 64, 128, KT, 128])
    f2_d = din("f2", [4, KT, 128, 64, 128])

    yT = dout("yT", [KT, 128, NTOK])
    kT = dout("kT", [KT, 128, NTOK])
    vO = dout("vO", [NTOK, D])
    shO = dout("shO", [2, 2, 128, KT])
    wkvO = dout("wkvO", [2, 2, 128, KT, 64])

    X = dint("Xs", [KT, 128, NTOK])
    VF = dint("VFs", [KT, 128, NTOK])
    KTs = dint("KTs", [KT, 128, NTOK], BF16)
    Vs = dint("Vss", [NTOK, D], BF16)

    with es:
        S = Sched(nc, es)
        SBN = 206 * 1024
        sb_t = es.enter_context(nc.sbuf_tensor("sb", [128, SBN // 4], F32))
        ps_t = es.enter_context(nc.psum_tensor("ps", [128, 4096], F32))
        sb = Arena(sb_t[:, :], "sb", SBN)

        def psb(bank, nb=1):
            return V(ps_t[:, bank * 512:(bank + nb) * 512], "ps", bank * 2048, (bank + nb) * 2048)

        def psb16(bank):
            return V(ps_t[:, bank * 512:(bank + 1) * 512].bitcast(BF16), "ps", bank * 2048, (bank + 1) * 2048)

        def dres(name, lo=0, hi=1 << 30):
            return V(None, name, lo, hi)

        CON = sb.alloc(F32, [128, NCONST])
        VEC = sb.alloc(F32, [128, NV, KT])
        OMK = sb.alloc(F32, [128, 2, KT])
        IDB = sb.alloc(BF16, [128, 128])
        ONB = sb.alloc(BF16, [128, 128])
        BLB = sb.alloc(BF16, [128, 128])
        ROB = sb.alloc(BF16, [128, 128])
        M4B = sb.alloc(BF16, [128, 4, 64])
        MSB = sb.alloc(BF16, [128, 64])
        IDR = sb.alloc(BF16, [128, 8, 64])
        ST = sb.alloc(F32, [128, KT, 64])
        STB = sb.alloc(BF16, [128, KT, 64])
        CAR = sb.alloc(F32, [128, KT])
        LAMV = sb.alloc(F32, [128, 8])
        SUBW = sb.alloc(F32, [128, 2, 256])
        SQ = [sb.alloc(BF16, [128, TT]) for _ in range(2)]
        RSTD = sb.alloc(F32, [128, TT])
        COS = sb.alloc(F32, [128, TT])
        SIN = sb.alloc(F32, [128, TT])
        base_pos = sb.pos
        cur = {"NA": TT}

        dbg_list = []

        def dbg(name, v, ap, shape, dt=F32):
            if not cfg.get("debug"):
                return
            d = nc.dram_tensor("dbg_" + name, list(shape), dt, kind="ExternalOutput").ap()
            S.dma("sp", d, ap, reads=[v], slot="dbg")

        def cc(name, p=128):
            o, n = CL[name]
            return CON[0:p, o:o + n]

        S.dma("sp", CON[:, :], consts_d, writes=[CON], slot="con")
        S.dma("sp", VEC[:, :, :], vecs_d, writes=[VEC], slot="vec")
        S.dma("sp", SUBW[:, :, :], subw_d, writes=[SUBW], slot="subw")
        for dst, name in [(IDB, "ident"), (ONB, "ones"), (BLB, "blk"), (ROB, "rot")]:
            S.op("dve", lambda e, dst=dst, name=name: e.tensor_copy(out=dst[:, :], in_=cc(name)), reads=[CON], writes=[dst])
        S.op("dve", lambda e: e.tensor_copy(out=M4B[:, :, :], in_=cc("m4").rearrange("p (a b) -> p a b", a=4)), reads=[CON], writes=[M4B])
        S.op("dve", lambda e: e.tensor_copy(out=MSB[:, :], in_=cc("ms")), reads=[CON], writes=[MSB])
        S.op("dve", lambda e: e.tensor_copy(out=IDR[:, :, :], in_=cc("idrep")[:, 0:512].rearrange("p (a b) -> p a b", a=8)), reads=[CON], writes=[IDR])
        for l in range(2):
            S.op("dve", lambda e, l=l: e.tensor_scalar(out=OMK[:, l, :], in0=VEC[:, VEC_VC + l * 6 + 3, :], scalar1=-1.0, scalar2=1.0,
                                                    op0=ALU.mult, op1=ALU.add), reads=[VEC], writes=[OMK])

        def vcol(idx, kt):
            return VEC[:, idx, kt:kt + 1]

        EPS_N = cc("eps")[:, 0:1]
        EPS_G = cc("eps")[:, 1:2]
        EPS_S = cc("eps")[:, 2:3]
        EPS_K = cc("eps")[:, 3:4]

        PB = [psb(i) for i in range(8)]

        def rms_stats(src, n, pbank, scale, eps_ap, nk=KT, blockones=False, srcs=None):
            lhs = BLB if blockones else ONB
            for kt in range(nk):
                sq = SQ[kt % 2]
                S.op("act", lambda e, kt=kt, sq=sq: e.activation(out=sq[:, 0:n], in_=src[:, kt, 0:n], func=AF.Square),
                     reads=[src], writes=[sq])
                S.op("pe", lambda e, kt=kt, sq=sq: e.matmul(pbank[:, 0:n], lhsT=lhs[:, :], rhs=sq[:, 0:n], start=(kt == 0), stop=(kt == nk - 1)),
                     reads=[sq, lhs], writes=[pbank])
            S.op("act", lambda e: e.activation(out=RSTD[:, 0:n], in_=pbank[:, 0:n], func=AF.Sqrt, bias=eps_ap, scale=scale),
                 reads=[pbank, CON], writes=[RSTD])
            S.op("dve", lambda e: e.reciprocal(out=RSTD[:, 0:n], in_=RSTD[:, 0:n]), reads=[RSTD], writes=[RSTD])

        def rms_apply(dst, src, gidx, n, doff=0):
            for kt in range(KT):
                S.op("dve", lambda e, kt=kt: e.scalar_tensor_tensor(out=dst[:, kt, doff:doff + n], in0=src[:, kt, 0:n], scalar=vcol(gidx, kt),
                                                                    in1=RSTD[:, 0:n], op0=ALU.mult, op1=ALU.mult),
                     reads=[src, RSTD, VEC], writes=[dst])

        def load_x(dst, l, t0, n):
            srcd = x0 if l == 0 else X
            S.dma("sp", dst[:, :, 0:n], srcd[:, :, t0:t0 + n].rearrange("k p t -> p k t"),
                  reads=[dres("X", t0, t0 + n)] if l > 0 else [], writes=[dst], slot="ldx")

        W16 = {}
        wdone = {}
        pending_wb = []

        def w16(t32):
            nm = t32.name
            if nm not in W16:
                W16[nm] = dint("c16_" + nm, list(t32.shape), BF16)
            return W16[nm]

        def flush_wb(keep=0):
            while len(pending_wb) > keep:
                dst_v, dap, rid, slot = pending_wb.pop(0)
                S.dma("pool", dap, dst_v.ap, reads=[dst_v], writes=[dres("w16", rid, rid + 1)], slot=("wb", slot))

        def wload(dst_v, t32, idx, slot, key):
            if not cfg.get("w16", True):
                S.dma("pool", None, None, writes=[dst_v], slot=slot,
                      fn=lambda e: e.dma_start(out=dst_v.ap, in_=idx(t32), max_dma_last_dim=8192))
                return
            key = (t32.name,) + tuple(key)
            if any(p[0].lo < dst_v.hi and dst_v.lo < p[0].hi for p in pending_wb):
                flush_wb(0)
            if key not in wdone:
                rid = len(wdone)
                wdone[key] = rid
                S.dma("pool", None, None, writes=[dst_v], slot=slot,
                      fn=lambda e: e.dma_start(out=dst_v.ap, in_=idx(t32), max_dma_last_dim=8192))
                flush_wb(0)
                pending_wb.append((dst_v, idx(w16(t32)), rid, slot))
            else:
                rid = wdone[key]
                flush_wb(0)
                S.dma("sp", dst_v.ap, idx(w16(t32)), reads=[dres("w16", rid, rid + 1)], writes=[dst_v], slot=slot)

        def proj_fm(pbank, wv, rhs, n, nk=KT):
            S.group("pe", [lambda e, kt=kt: e.matmul(pbank[:, 0:n], lhsT=wv[:, kt, :], rhs=rhs[:, kt, 0:n], start=(kt == 0), stop=(kt == nk - 1))
                           for kt in range(nk)], reads=[wv, rhs], writes=[pbank])

        def rope_evac(pbank, n, t0, out_f32, tmpb, pb2):
            S.op("act", lambda e: e.activation(out=tmpb[:, 0:n], in_=pbank[:, 0:n], func=AF.Copy), reads=[pbank], writes=[tmpb])
            S.op("pe", lambda e: e.matmul(pb2[:, 0:n], lhsT=ROB[:, :], rhs=tmpb[:, 0:n], start=True, stop=True), reads=[ROB, tmpb], writes=[pb2])
            S.op("dve", lambda e: e.tensor_tensor(out=out_f32[:, 0:n], in0=pbank[:, 0:n], in1=COS[:, 0:n], op=ALU.mult),
                 reads=[pbank, COS], writes=[out_f32])
            S.op("dve", lambda e: e.tensor_tensor(out=tmpb[:, 0:n], in0=pb2[:, 0:n], in1=SIN[:, 0:n], op=ALU.mult),
                 reads=[pb2, SIN], writes=[tmpb])
            S.op("dve", lambda e: e.tensor_tensor(out=out_f32[:, 0:n], in0=out_f32[:, 0:n], in1=tmpb[:, 0:n], op=ALU.add),
                 reads=[tmpb, out_f32], writes=[out_f32])

        def load_rope(t0, n):
            S.dma("sp", COS[:, 0:n], cos_d[:, t0:t0 + n], writes=[COS], slot="rope")
            S.dma("sp", SIN[:, 0:n], sin_d[:, t0:t0 + n], writes=[SIN], slot="rope")

        def post_and_ffn(l, XT, HO, n, t0, ffn_pos):
            NA = cur["NA"]
            sb.pos = ffn_pos
            XNB = sb.alloc(BF16, [128, KT, NA])
            kv_pos = sb.pos
            H = sb.alloc(BF16, [128, 32, NA])
            R1 = [sb.alloc(BF16, [128, NA]) for _ in range(2)]
            W1 = [sb.alloc(BF16, [128, 4, KT, 128]) for _ in range(2)]
            W2 = [sb.alloc(BF16, [128, 32, 128]) for _ in range(2)]
            rms_stats(HO, n, PB[7], 1.0 / D, EPS_N)
            rms_apply(HO, HO, VEC_NM + l * 2 + 1, n)
            S.op("dve", lambda e: e.tensor_tensor(out=XT[:, :, 0:n], in0=XT[:, :, 0:n], in1=HO[:, :, 0:n], op=ALU.add),
                 reads=[XT, HO], writes=[XT])
            rms_stats(XT, n, PB[7], 1.0 / D, EPS_N)
            rms_apply(XNB, XT, VEC_NF + l * 2 + 0, n)
            for half in range(2):
                for g in range(8):
                    w1 = W1[g % 2]
                    wload(w1, f1_d, lambda t, l=l, half=half, g=g: t[l, half * 32 + g * 4:half * 32 + g * 4 + 4].rearrange("f p k c -> p f k c"), ("w1", g % 2), (l, half, g))
                    for f in range(4):
                        fc = g * 4 + f
                        pb = PB[fc % 2]
                        S.group("pe", [lambda e, kt=kt, f=f, w1=w1, pb=pb: e.matmul(pb[:, 0:n], lhsT=w1[:, f, kt, :], rhs=XNB[:, kt, 0:n],
                                                                            start=(kt == 0), stop=(kt == KT - 1)) for kt in range(KT)],
                                reads=[w1, XNB], writes=[pb])
                        r1 = R1[fc % 2]
                        S.op("act", lambda e, pb=pb, r1=r1: e.activation(out=r1[:, 0:n], in_=pb[:, 0:n], func=AF.Relu), reads=[pb], writes=[r1])
                        S.op("pool", lambda e, r1=r1, fc=fc: e.tensor_tensor(out=H[:, fc, 0:n], in0=r1[:, 0:n], in1=r1[:, 0:n], op=ALU.mult),
                             reads=[r1], writes=[sub(H, None, fc * NA * 2, (fc + 1) * NA * 2)])
                flush_wb(0)
                for oc in range(KT):
                    w2 = W2[oc % 2]
                    wload(w2, f2_d, lambda t, l=l, oc=oc, half=half: t[l, oc, :, half * 32:half * 32 + 32, :], ("w2", oc % 2), (l, oc, half))
                    pb = PB[2 + oc % 2]
                    S.group("pe", [lambda e, fk=fk, w2=w2, pb=pb: e.matmul(pb[:, 0:n], lhsT=w2[:, fk, :], rhs=H[:, fk, 0:n],
                                                                        start=(fk == 0), stop=(fk == 31)) for fk in range(32)],
                            reads=[w2, H], writes=[pb])
                    hov = sub(HO, None, oc * NA * 4, (oc + 1) * NA * 4)
                    if half == 0:
                        S.op("act", lambda e, pb=pb, oc=oc: e.activation(out=HO[:, oc, 0:n], in_=pb[:, 0:n], func=AF.Copy), reads=[pb], writes=[hov])
                    else:
                        S.op("dve", lambda e, pb=pb, oc=oc: e.tensor_tensor(out=HO[:, oc, 0:n], in0=pb[:, 0:n], in1=HO[:, oc, 0:n], op=ALU.add),
                             reads=[pb, hov], writes=[hov])
            flush_wb(0)
            rms_stats(HO, n, PB[7], 1.0 / D, EPS_N)
            rms_apply(HO, HO, VEC_NF + l * 2 + 1, n)
            S.op("dve", lambda e: e.tensor_tensor(out=XT[:, :, 0:n], in0=XT[:, :, 0:n], in1=HO[:, :, 0:n], op=ALU.add),
                 reads=[XT, HO], writes=[XT])
            dst = yT if l == 3 else X
            S.dma("sp", dst[:, :, t0:t0 + n].rearrange("k p t -> p k t"), XT[:, :, 0:n], reads=[XT],
                  writes=[dres("X", t0, t0 + n)] if l < 3 else [], slot="stx")
            if cfg.get("stop") == "ffn" and l == cfg.get("stop_layer", 0):
                raise StopBuild()
            if l == 1:
                kv_proj(XT, XNB, n, t0, kv_pos)

        def kv_proj(XT, XNB, n, t0, kv_pos):
            NA = cur["NA"]
            mark = sb.pos
            sb.pos = kv_pos
            WK = [sb.alloc(BF16, [128, KT, 128]) for _ in range(2)]
            WVV = sb.alloc(BF16, [128, KT, 512])
            KF = [sb.alloc(F32, [128, NA]) for _ in range(2)]
            KB = [sb.alloc(BF16, [128, NA]) for _ in range(2)]
            TB = sb.alloc(BF16, [128, NA])
            VT = [sb.alloc(F32, [128, 512]) for _ in range(2)]
            VTB = [sb.alloc(BF16, [128, 512]) for _ in range(2)]
            rms_stats(XT, n, PB[7], 1.0 / D, EPS_N)
            rms_apply(XNB, XT, VEC_KVN, n)
            load_rope(t0, n)
            if cfg.get("stop") == "kv0":
                raise StopBuild()
            for hm in range(KT):
                wk = WK[hm % 2]
                wload(wk, kvk_d, lambda t, hm=hm: t[hm], ("wkv", hm % 2), (hm,))
                pb = PB[hm % 2]
                proj_fm(pb, wk, XNB, n)
                kf, kb = KF[hm % 2], KB[hm % 2]
                if cfg.get("stop") == "kv1":
                    raise StopBuild()
                rope_evac(pb, n, t0, kf, TB, PB[2 + hm % 2])
                if cfg.get("stop") == "kv2":
                    raise StopBuild()
                S.op("act", lambda e, kf=kf, kb=kb: e.activation(out=kb[:, 0:n], in_=kf[:, 0:n], func=AF.Copy), reads=[kf], writes=[kb])
                S.dma("sp", kT[hm, :, t0:t0 + n], kf[:, 0:n], reads=[kf], slot=("kfo", hm % 2))
                S.dma("sp", KTs[hm, :, t0:t0 + n], kb[:, 0:n], reads=[kb], writes=[dres("KTs", hm * 4096 + t0, hm * 4096 + t0 + n)], slot=("kbo", hm % 2))
            flush_wb(0)
            if cfg.get("stop") == "kvk":
                raise StopBuild()
            ntt = (n + 127) // 128
            for fcg in range(4):
                wload(WVV, kvv_d, lambda t, fcg=fcg: t[fcg], "wvv", (fcg,))
                for tt in range(ntt):
                    m = min(128, n - tt * 128)
                    i = (fcg * ntt + tt) % 2
                    pb = PB[4 + i]
                    S.group("pe", [lambda e, kt=kt, tt=tt, m=m, pb=pb: e.matmul(pb[0:m, :], lhsT=XNB[:, kt, tt * 128:tt * 128 + m], rhs=WVV[:, kt, :],
                                                                             start=(kt == 0), stop=(kt == KT - 1)) for kt in range(KT)],
                            reads=[XNB, WVV], writes=[pb])
                    vt, vtb = VT[i], VTB[i]
                    S.op("act", lambda e, vt=vt, pb=pb, m=m: e.activation(out=vt[0:m, :], in_=pb[0:m, :], func=AF.Copy), reads=[pb], writes=[vt])
                    S.op("dve", lambda e, vt=vt, vtb=vtb, m=m: e.tensor_copy(out=vtb[0:m, :], in_=vt[0:m, :]), reads=[vt], writes=[vtb])
                    r0 = t0 + tt * 128
                    S.dma("sp", vO[r0:r0 + m, fcg * 512:(fcg + 1) * 512], vt[0:m, :], reads=[vt], slot=("vfo", i))
                    S.dma("sp", Vs[r0:r0 + m, fcg * 512:(fcg + 1) * 512], vtb[0:m, :], reads=[vtb], writes=[dres("Vs", r0 * 4 + fcg, r0 * 4 + fcg + 1)], slot=("vbo", i))
            flush_wb(0)
            sb.pos = mark

        def rwkv_tile(l, t0, n, first, last, sample):
            Lc = 8 if sample else LCH
            nch = n // Lc
            nlev = int(math.log2(Lc)) - 1
            NA = NS if sample else TT
            cur["NA"] = NA
            sb.pos = base_pos
            XT = sb.alloc(F32, [128, KT, NA])
            region_pos = sb.pos
            XN = sb.alloc(F32, [128, KT, NA + 1])
            XXT = [sb.alloc(F32, [128, NA]) for _ in range(2)]
            regA_end = sb.pos
            TMPB = sb.alloc(BF16, [128, KT, NA])
            MIX = [sb.alloc(BF16, [128, KT, NA]) for _ in range(3)]
            HW = sb.alloc(BF16, [96, NA])
            HA = sb.alloc(BF16, [96, NA])
            HV = sb.alloc(BF16, [64, NA])
            HG = sb.alloc(BF16, [128, 2, NA])
            l1_pos = sb.pos
            L1 = [sb.alloc(BF16, [128, KT, 96]), sb.alloc(BF16, [128, KT, 96]), sb.alloc(BF16, [128, KT, 64]), sb.alloc(BF16, [128, KT, 256])]
            sb.pos = l1_pos
            WCH = [[sb.alloc(BF16, [128, KT, 128]) for _ in range(3)] for _ in range(2)]
            L2C = [[sb.alloc(BF16, [96, 128]), sb.alloc(BF16, [96, 128]), sb.alloc(BF16, [64, 128]), sb.alloc(BF16, [128, 2, 128])] for _ in range(2)]
            p2_pos = sb.pos

            load_x(XT, l, t0, n)
            if first:
                if sample:
                    S.dma("sp", CAR[:, :], shift0[l], writes=[CAR], slot="car")
                    S.dma("sp", ST[:, :, :], wkv0[l], writes=[ST], slot="st0")
                    S.op("act", lambda e: e.activation(out=STB[:, :, :], in_=ST[:, :, :], func=AF.Copy), reads=[ST], writes=[STB])
                else:
                    S.op("pool", lambda e: e.memset(CAR[:, :], 0.0), writes=[CAR])
                    S.op("pool", lambda e: e.memset(ST[:, :, :], 0.0), writes=[ST])
                    S.op("pool", lambda e: e.memset(STB[:, :, :], 0.0), writes=[STB])
            for i, (dsrc, li, dst) in enumerate([(lw1_d, l, L1[0]), (la1_d, l, L1[1]), (lv1_d, 0, L1[2]), (lg1_d, l, L1[3])]):
                if i == 2 and l == 0:
                    continue
                wload(dst, dsrc, lambda t, li=li: t[li], ("l1", i), (li,))
            flush_wb(0)
            rms_stats(XT, n, PB[7], 1.0 / D, EPS_N)
            S.op("dve", lambda e: e.tensor_copy(out=XN[:, :, 0:1], in_=CAR[:, :].unsqueeze(2)), reads=[CAR], writes=[XN])
            rms_apply(XN, XT, VEC_NM + l * 2 + 0, n, doff=1)
            S.op("dve", lambda e: e.tensor_copy(out=CAR[:, :].unsqueeze(2), in_=XN[:, :, n:n + 1]), reads=[XN], writes=[CAR])
            if last:
                S.dma("sp", shO[l, 1 if sample else 0], CAR[:, :], reads=[CAR], slot="sho")

            def mix(i, dst):
                for kt in range(KT):
                    xx = XXT[kt % 2]
                    S.op("pool", lambda e, kt=kt, xx=xx: e.tensor_tensor(out=xx[:, 0:n], in0=XN[:, kt, 0:n], in1=XN[:, kt, 1:n + 1], op=ALU.subtract),
                         reads=[XN], writes=[xx])
                    S.op("dve", lambda e, kt=kt, xx=xx: e.scalar_tensor_tensor(out=dst[:, kt, 0:n], in0=xx[:, 0:n], scalar=vcol(VEC_MU + l * 6 + i, kt),
                                                                               in1=XN[:, kt, 1:n + 1], op0=ALU.mult, op1=ALU.add),
                         reads=[xx, XN, VEC], writes=[dst])

            def lora1(src, w, r, outv, func, pb, oc=None):
                cs = slice(0, r) if oc is None else slice(oc * 128, oc * 128 + 128)
                rr = r if oc is None else 128
                S.group("pe", [lambda e, kt=kt: e.matmul(pb[0:rr, 0:n], lhsT=w[:, kt, cs], rhs=src[:, kt, 0:n], start=(kt == 0), stop=(kt == KT - 1))
                               for kt in range(KT)], reads=[w, src], writes=[pb])
                oap = outv[0:rr, 0:n] if oc is None else outv[:, oc, 0:n]
                S.op("act", lambda e: e.activation(out=oap, in_=pb[0:rr, 0:n], func=func), reads=[pb], writes=[outv])

            mix(1, TMPB)
            lora1(TMPB, L1[0], 96, HW, AF.Tanh, PB[0])
            mix(4, TMPB)
            lora1(TMPB, L1[1], 96, HA, AF.Copy, PB[1])
            mix(5, TMPB)
            lora1(TMPB, L1[3], 256, HG, AF.Sigmoid, PB[0], oc=0)
            lora1(TMPB, L1[3], 256, HG, AF.Sigmoid, PB[1], oc=1)
            mix(0, MIX[0])
            mix(2, MIX[1])
            mix(3, MIX[2])
            if l == 1:
                lora1(MIX[2], L1[2], 64, HV, AF.Copy, PB[0])
            YG = TMPB
            if cfg.get("stop") == "p1" and l == cfg.get("stop_layer", 0):
                raise StopBuild()

            sb.pos = p2_pos if sample else base_pos
            Rf = sb.alloc(F32, [128, NA]); Kf = sb.alloc(F32, [128, NA]); Vf = sb.alloc(F32, [128, NA])
            LW = sb.alloc(F32, [128, NA]); AS = sb.alloc(F32, [128, NA]); KK = sb.alloc(F32, [128, NA])
            T1 = sb.alloc(F32, [128, NA]); T2 = sb.alloc(F32, [128, NA]); CLs = sb.alloc(F32, [128, NA])
            E1 = sb.alloc(F32, [128, NA]); E2 = sb.alloc(F32, [128, NA]); Gf = sb.alloc(F32, [128, NA])
            Cc = sb.alloc(F32, [128, NA]); Yf = sb.alloc(F32, [128, NA]); VF1 = sb.alloc(F32, [128, NA])
            T3B = sb.alloc(BF16, [128, NA])
            AR = sb.alloc(BF16, [128, nch, 2, Lc]); BK = sb.alloc(BF16, [128, nch, 2, Lc]); VB = sb.alloc(BF16, [128, NA])
            TM = sb.alloc(BF16, [128, nch, 3, 64])
            MTS = sb.alloc(BF16, [128, nch, 4, Lc])
            PT0 = sb.alloc(BF16, [128, nch, Lc])
            PP = [sb.alloc(BF16, [128, nch, Lc]) for _ in range(2)]
            PPT = [sb.alloc(BF16, [128, nch, Lc]) for _ in range(2)]
            WW = [sb.alloc(BF16, [128, nch, Lc]) for _ in range(2)]
            XS = sb.alloc(BF16, [128, 64]); US = sb.alloc(BF16, [128, 64])
            X2S = sb.alloc(BF16, [128, nch, 64])
            assert sb.pos <= regA_end or sample, (sb.pos, regA_end)

            vc = lambda i, hp: VEC[:, VEC_VC + l * 6 + i, hp:hp + 1]

            def load_hp(hp):
                s = hp % 2
                wload(WCH[s][0], wr_d, lambda t, hp=hp: t[l, hp], ("wch", s), (l, hp))
                wload(WCH[s][1], wk_d, lambda t, hp=hp: t[l, hp], ("wch", s), (l, hp))
                wload(WCH[s][2], wv_d, lambda t, hp=hp: t[l, hp], ("wch", s), (l, hp))
                S.dma("pool", L2C[s][0][:, :], lw2_d[l, :, hp * 128:(hp + 1) * 128], writes=[L2C[s][0]], slot=("l2c", s))
                S.dma("pool", L2C[s][1][:, :], la2_d[l, :, hp * 128:(hp + 1) * 128], writes=[L2C[s][1]], slot=("l2c", s))
                if l == 1:
                    S.dma("pool", L2C[s][2][:, :], lv2_d[0, :, hp * 128:(hp + 1) * 128], writes=[L2C[s][2]], slot=("l2c", s))
                S.dma("pool", L2C[s][3][:, :, :], lg2_d[l, :, :, hp * 128:(hp + 1) * 128], writes=[L2C[s][3]], slot=("l2c", s))

            load_hp(0)
            for hp in range(KT):
                s = hp % 2
                if hp + 1 < KT:
                    load_hp(hp + 1)
                wr, wk, wv = WCH[s]
                w2c, a2c, v2c, g2c = L2C[s]
                proj_fm(PB[0], wr, MIX[0], n)
                S.op("act", lambda e: e.activation(out=Rf[:, 0:n], in_=PB[0][:, 0:n], func=AF.Copy), reads=[PB[0]], writes=[Rf])
                proj_fm(PB[1], wk, MIX[1], n)
                S.op("act", lambda e: e.activation(out=Kf[:, 0:n], in_=PB[1][:, 0:n], func=AF.Copy), reads=[PB[1]], writes=[Kf])
                proj_fm(PB[0], wv, MIX[2], n)
                S.op("act", lambda e: e.activation(out=Vf[:, 0:n], in_=PB[0][:, 0:n], func=AF.Copy), reads=[PB[0]], writes=[Vf])
                S.op("pe", lambda e: e.matmul(PB[1][:, 0:n], lhsT=w2c[:, :], rhs=HW[:, 0:n], start=True, stop=True), reads=[w2c, HW], writes=[PB[1]])
                S.op("act", lambda e, hp=hp: e.activation(out=LW[:, 0:n], in_=PB[1][:, 0:n], func=AF.Sigmoid, bias=vc(0, hp), scale=1.0),
                     reads=[PB[1], VEC], writes=[LW])
                S.op("dve", lambda e: e.tensor_scalar(out=LW[:, 0:n], in0=LW[:, 0:n], scalar1=-math.exp(-0.5), scalar2=None, op0=ALU.mult),
                     reads=[LW], writes=[LW])
                S.op("pe", lambda e: e.matmul(PB[0][:, 0:n], lhsT=a2c[:, :], rhs=HA[:, 0:n], start=True, stop=True), reads=[a2c, HA], writes=[PB[0]])
                S.op("act", lambda e, hp=hp: e.activation(out=AS[:, 0:n], in_=PB[0][:, 0:n], func=AF.Sigmoid, bias=vc(1, hp), scale=1.0),
                     reads=[PB[0], VEC], writes=[AS])
                S.group("pe", [lambda e, j=j: e.matmul(PB[1][:, 0:n], lhsT=g2c[:, j, :], rhs=HG[:, j, 0:n], start=(j == 0), stop=(j == 1)) for j in range(2)],
                        reads=[g2c, HG], writes=[PB[1]])
                S.op("act", lambda e: e.activation(out=Gf[:, 0:n], in_=PB[1][:, 0:n], func=AF.Copy), reads=[PB[1]], writes=[Gf])
                vfd = VF[hp, :, t0:t0 + n]
                if l == 0:
                    S.dma("sp", vfd, Vf[:, 0:n], reads=[Vf], writes=[dres("VF", hp, hp + 1)], slot="vfst")
                else:
                    S.dma("sp", VF1[:, 0:n], vfd, reads=[dres("VF", hp, hp + 1)], writes=[VF1], slot="vfld")
                    S.op("pe", lambda e: e.matmul(PB[0][:, 0:n], lhsT=v2c[:, :], rhs=HV[:, 0:n], start=True, stop=True), reads=[v2c, HV], writes=[PB[0]])
                    S.op("act", lambda e, hp=hp: e.activation(out=T1[:, 0:n], in_=PB[0][:, 0:n], func=AF.Sigmoid, bias=VEC[:, VEC_V0, hp:hp + 1], scale=1.0),
                         reads=[PB[0], VEC], writes=[T1])
                    S.op("dve", lambda e: e.tensor_tensor(out=T2[:, 0:n], in0=VF1[:, 0:n], in1=Vf[:, 0:n], op=ALU.subtract), reads=[VF1, Vf], writes=[T2])
                    S.op("dve", lambda e: e.tensor_tensor(out=T2[:, 0:n], in0=T2[:, 0:n], in1=T1[:, 0:n], op=ALU.mult), reads=[T2, T1], writes=[T2])
                    S.op("dve", lambda e: e.tensor_tensor(out=Vf[:, 0:n], in0=Vf[:, 0:n], in1=T2[:, 0:n], op=ALU.add), reads=[Vf, T2], writes=[Vf])
                S.op("dve", lambda e, hp=hp: e.tensor_scalar(out=KK[:, 0:n], in0=Kf[:, 0:n], scalar1=vc(2, hp), scalar2=None, op0=ALU.mult),
                     reads=[Kf, VEC], writes=[KK])
                S.op("act", lambda e: e.activation(out=T3B[:, 0:n], in_=KK[:, 0:n], func=AF.Square), reads=[KK], writes=[T3B])
                S.op("pe", lambda e: e.matmul(PB[0][:, 0:n], lhsT=BLB[:, :], rhs=T3B[:, 0:n], start=True, stop=True), reads=[BLB, T3B], writes=[PB[0]])
                S.op("act", lambda e: e.activation(out=T1[:, 0:n], in_=PB[0][:, 0:n], func=AF.Sqrt), reads=[PB[0]], writes=[T1])
                S.op("dve", lambda e: e.tensor_scalar(out=T1[:, 0:n], in0=T1[:, 0:n], scalar1=1e-12, scalar2=None, op0=ALU.max), reads=[T1], writes=[T1])
                S.op("dve", lambda e: e.reciprocal(out=T1[:, 0:n], in_=T1[:, 0:n]), reads=[T1], writes=[T1])
                S.op("dve", lambda e: e.tensor_tensor(out=KK[:, 0:n], in0=KK[:, 0:n], in1=T1[:, 0:n], op=ALU.mult), reads=[KK, T1], writes=[KK])
                S.op("dve", lambda e, hp=hp: e.tensor_scalar(out=T1[:, 0:n], in0=AS[:, 0:n], scalar1=vc(3, hp), scalar2=OMK[:, l, hp:hp + 1],
                                                            op0=ALU.mult, op1=ALU.add), reads=[AS, VEC, OMK], writes=[T1])
                S.op("dve", lambda e: e.tensor_tensor(out=Kf[:, 0:n], in0=Kf[:, 0:n], in1=T1[:, 0:n], op=ALU.mult), reads=[Kf, T1], writes=[Kf])
                S.op("dve", lambda e, hp=hp: e.scalar_tensor_tensor(out=T3B[:, 0:n], in0=Rf[:, 0:n], scalar=VEC[:, VEC_RK + l, hp:hp + 1], in1=Kf[:, 0:n],
                                                                   op0=ALU.mult, op1=ALU.mult), reads=[Rf, Kf, VEC], writes=[T3B])
                S.op("pe", lambda e: e.matmul(PB[1][:, 0:n], lhsT=BLB[:, :], rhs=T3B[:, 0:n], start=True, stop=True), reads=[BLB, T3B], writes=[PB[1]])
                S.op("act", lambda e: e.activation(out=Cc[:, 0:n], in_=PB[1][:, 0:n], func=AF.Copy), reads=[PB[1]], writes=[Cc])
                rst = cc("reset")[:, 0:n] if not sample else cc("reset")[:, 0:n]
                S.op("dve", lambda e: e.tensor_tensor_scan(out=CLs[:, 0:n], data0=rst, data1=LW[:, 0:n], initial=0.0, op0=ALU.mult, op1=ALU.add),
                     reads=[LW, CON], writes=[CLs])
                S.op("act", lambda e: e.activation(out=E1[:, 0:n], in_=CLs[:, 0:n], func=AF.Exp), reads=[CLs], writes=[E1])
                S.op("act", lambda e: e.activation(out=E2[:, 0:n], in_=CLs[:, 0:n], func=AF.Exp, scale=-1.0), reads=[CLs], writes=[E2])
                S.op("dve", lambda e: e.tensor_tensor(out=T1[:, 0:n], in0=CLs[:, 0:n], in1=LW[:, 0:n], op=ALU.subtract), reads=[CLs, LW], writes=[T1])
                S.op("act", lambda e: e.activation(out=T1[:, 0:n], in_=T1[:, 0:n], func=AF.Exp), reads=[T1], writes=[T1])
                v3 = lambda ap: ap.rearrange("p (c t) -> p c t", t=Lc)
                S.op("dve", lambda e: e.tensor_tensor(out=AR[:, :, 1, :], in0=v3(Rf[:, 0:n]), in1=v3(E1[:, 0:n]), op=ALU.mult), reads=[Rf, E1], writes=[AR])
                S.op("dve", lambda e: e.scalar_tensor_tensor(out=AR[:, :, 0, :], in0=v3(KK[:, 0:n]), scalar=-1.0, in1=v3(T1[:, 0:n]), op0=ALU.mult, op1=ALU.mult),
                     reads=[KK, T1], writes=[AR])
                S.op("dve", lambda e: e.tensor_tensor(out=T2[:, 0:n], in0=KK[:, 0:n], in1=AS[:, 0:n], op=ALU.mult), reads=[KK, AS], writes=[T2])
                S.op("dve", lambda e: e.tensor_tensor(out=BK[:, :, 0, :], in0=v3(T2[:, 0:n]), in1=v3(E2[:, 0:n]), op=ALU.mult), reads=[T2, E2], writes=[BK])
                S.op("dve", lambda e: e.tensor_tensor(out=BK[:, :, 1, :], in0=v3(Kf[:, 0:n]), in1=v3(E2[:, 0:n]), op=ALU.mult), reads=[Kf, E2], writes=[BK])
                S.op("act", lambda e: e.activation(out=VB[:, 0:n], in_=Vf[:, 0:n], func=AF.Copy), reads=[Vf], writes=[VB])
                HH = [(hh, slice(hh * 64, hh * 64 + 64), slice(hh * 64, hh * 64 + Lc)) for hh in range(2)]
                PR = [slice(0, 128)] if Lc == 64 else [slice(0, Lc), slice(64, 64 + Lc)]
                for c in range(nch):
                    pbt = psb16(2 + c % 2)
                    fns = []
                    for (hh, hs, hl) in HH:
                        for i, srcap in enumerate([BK[hs, c, 0, :], BK[hs, c, 1, :], VB[hs, c * Lc:(c + 1) * Lc]]):
                            fns.append(lambda e, i=i, srcap=srcap, hs=hs, hl=hl: e.transpose(pbt[hl, i * 64:(i + 1) * 64], srcap, IDB[hs, hs]))
                    S.group("pe", fns, reads=[BK, VB, IDB], writes=[pbt])
                    for pr in PR:
                        S.op("act", lambda e, c=c, pbt=pbt, pr=pr: e.activation(out=TM[pr, c, :, :], in_=pbt[pr, 0:192].rearrange("p (a b) -> p a b", a=3), func=AF.Copy),
                             reads=[pbt], writes=[TM])
                for c in range(nch):
                    pbm = PB[4 + c % 2]
                    fns = []
                    for (hh, hs, hl) in HH:
                        fns.append(lambda e, c=c, hs=hs, hl=hl: e.matmul(pbm[hl, 0:2 * Lc], lhsT=BK[hs, c, 0, :], rhs=AR[hs, c, :, :], start=True, stop=True))
                        fns.append(lambda e, c=c, hs=hs, hl=hl: e.matmul(pbm[hl, 2 * Lc:4 * Lc], lhsT=BK[hs, c, 1, :], rhs=AR[hs, c, :, :], start=True, stop=True))
                        fns.append(lambda e, c=c, hs=hs, hl=hl: e.matmul(pbm[hl, 4 * Lc:5 * Lc], lhsT=AR[hs, c, 0, :], rhs=BK[hs, c, 0, :], start=True, stop=True))
                    S.group("pe", fns, reads=[AR, BK], writes=[pbm])
                    for pr in PR:
                        S.op("dve", lambda e, c=c, pbm=pbm, pr=pr: e.tensor_tensor(out=MTS[pr, c, :, :], in0=pbm[pr, 0:4 * Lc].rearrange("p (a b) -> p a b", a=4),
                                                                              in1=M4B[pr, :, 0:Lc], op=ALU.mult), reads=[pbm, M4B], writes=[MTS])
                        S.op("dve", lambda e, c=c, pbm=pbm, pr=pr: e.tensor_tensor(out=PT0[pr, c, :], in0=pbm[pr, 4 * Lc:5 * Lc], in1=MSB[pr, 0:Lc], op=ALU.mult),
                             reads=[pbm, MSB], writes=[PT0])
                for c0 in range(0, nch, 8):
                    c1 = min(nch, c0 + 8)
                    pbx = PB[6]
                    S.group("pe", [lambda e, c=c, hl=hl: e.matmul(pbx[hl, (c - c0) * 64:(c - c0 + 1) * 64], lhsT=MTS[hl, c, 2, :], rhs=TM[hl, c, 2, :], start=True, stop=True)
                                   for c in range(c0, c1) for (hh, hs, hl) in HH], reads=[MTS, TM], writes=[pbx])
                    for pr in PR:
                        S.op("act", lambda e, pr=pr: e.activation(out=X2S[pr, c0:c1, :], in_=pbx[pr, 0:(c1 - c0) * 64].rearrange("p (a b) -> p a b", b=64), func=AF.Copy),
                             reads=[pbx], writes=[X2S])
                for pr in PR:
                    S.op("dve", lambda e, pr=pr: e.tensor_tensor(out=WW[0][pr, :, :], in0=MTS[pr, :, 0, :], in1=IDR[pr, 0:nch, 0:Lc], op=ALU.add),
                         reads=[MTS, IDR], writes=[WW[0]])
                for lev in range(1, nlev + 1):
                    Pp = (lambda hl, b: MTS[hl, b, 0, :]) if lev == 1 else (lambda hl, b, q=PP[(lev - 1) % 2]: q[hl, b, :])
                    PTp = (lambda hl, b: PT0[hl, b, :]) if lev == 1 else (lambda hl, b, q=PPT[(lev - 1) % 2]: q[hl, b, :])
                    Pp_v = MTS if lev == 1 else PP[(lev - 1) % 2]
                    PTp_v = PT0 if lev == 1 else PPT[(lev - 1) % 2]
                    Pn, PTn = PP[lev % 2], PPT[lev % 2]
                    Wp, Wn = WW[(lev - 1) % 2], WW[lev % 2]
                    pa, pbk, pw = PB[4], PB[5], PB[6]
                    if lev < nlev:
                        S.group("pe", [lambda e, b=b, hl=hl: e.matmul(pa[hl, b * Lc:(b + 1) * Lc], lhsT=PTp(hl, b), rhs=Pp(hl, b), start=True, stop=True)
                                       for b in range(nch) for (hh, hs, hl) in HH], reads=[Pp_v, PTp_v], writes=[pa])
                        for pr in PR:
                            S.op("act", lambda e, pr=pr: e.activation(out=Pn[pr, :, :], in_=pa[pr, 0:nch * Lc].rearrange("p (a b) -> p a b", b=Lc), func=AF.Copy),
                                 reads=[pa], writes=[Pn])
                    S.group("pe", [lambda e, b=b, hl=hl: e.matmul(pbk[hl, b * Lc:(b + 1) * Lc], lhsT=Pp(hl, b), rhs=PTp(hl, b), start=True, stop=True)
                                   for b in range(nch) for (hh, hs, hl) in HH], reads=[Pp_v, PTp_v], writes=[pbk])
                    for pr in PR:
                        S.op("act", lambda e, pr=pr: e.activation(out=PTn[pr, :, :], in_=pbk[pr, 0:nch * Lc].rearrange("p (a b) -> p a b", b=Lc), func=AF.Copy),
                             reads=[pbk], writes=[PTn])
                    S.group("pe", [lambda e, b=b, hl=hl: e.matmul(pw[hl, b * Lc:(b + 1) * Lc], lhsT=PTn[hl, b, :], rhs=Wp[hl, b, :], start=True, stop=True)
                                   for b in range(nch) for (hh, hs, hl) in HH], reads=[PTn, Wp], writes=[pw])
                    for pr in PR:
                        S.op("dve", lambda e, pr=pr: e.tensor_tensor(out=Wn[pr, :, :], in0=pw[pr, 0:nch * Lc].rearrange("p (a b) -> p a b", b=Lc),
                                                                  in1=Wp[pr, :, :], op=ALU.add), reads=[pw, Wp], writes=[Wn])
                NT = WW[nlev % 2]
                if cfg.get("stop") == "pre" and l == cfg.get("stop_layer", 0):
                    raise StopBuild()
                ypb = PB[3]
                for c in range(nch):
                    px, pu, psn = PB[4 + c % 2], PB[6], PB[7]
                    S.group("pe", [lambda e, hs=hs, hl=hl, c=c: e.matmul(px[hl, 0:64], lhsT=AR[hs, c, 0, :], rhs=STB[hs, hp, :], start=True, stop=True)
                                   for (hh, hs, hl) in HH], reads=[AR, STB], writes=[px])
                    for pr in PR:
                        S.op("dve", lambda e, px=px, c=c, pr=pr: e.tensor_tensor(out=XS[pr, :], in0=px[pr, 0:64], in1=X2S[pr, c, :], op=ALU.add), reads=[px, X2S], writes=[XS])
                    S.group("pe", [lambda e, hl=hl, c=c: e.matmul(pu[hl, 0:64], lhsT=NT[hl, c, :], rhs=XS[hl, :], start=True, stop=True)
                                   for (hh, hs, hl) in HH], reads=[NT, XS], writes=[pu])
                    for pr in PR:
                        S.op("act", lambda e, pr=pr: e.activation(out=US[pr, :], in_=pu[pr, 0:64], func=AF.Copy), reads=[pu], writes=[US])
                    fns = []
                    for (hh, hs, hl) in HH:
                        yo = lambda hs=hs, c=c: ypb[hs, c * Lc:(c + 1) * Lc]
                        fns.append(lambda e, hs=hs, c=c, yo=yo: e.matmul(yo(), lhsT=STB[hs, hp, :], rhs=AR[hs, c, 1, :], start=True, stop=False))
                        fns.append(lambda e, hl=hl, c=c, yo=yo: e.matmul(yo(), lhsT=US[hl, :], rhs=MTS[hl, c, 1, :], start=False, stop=False))
                        fns.append(lambda e, hl=hl, c=c, yo=yo: e.matmul(yo(), lhsT=TM[hl, c, 2, :], rhs=MTS[hl, c, 3, :], start=False, stop=True))
                    S.group("pe", fns, reads=[STB, AR, US, MTS, TM], writes=[ypb])
                    fns = []
                    for (hh, hs, hl) in HH:
                        fns.append(lambda e, hs=hs, hl=hl, c=c: e.matmul(psn[hs, 0:64], lhsT=TM[hl, c, 0, :], rhs=US[hl, :], start=True, stop=False))
                        fns.append(lambda e, hs=hs, hl=hl, c=c: e.matmul(psn[hs, 0:64], lhsT=TM[hl, c, 1, :], rhs=TM[hl, c, 2, :], start=False, stop=True))
                    S.group("pe", fns, reads=[TM, US], writes=[psn])
                    stv = sub(ST, None, hp * 256, (hp + 1) * 256)
                    stbv = sub(STB, None, hp * 128, (hp + 1) * 128)
                    S.op("dve", lambda e: e.tensor_tensor(out=ST[:, hp, :], in0=psn[:, 0:64], in1=ST[:, hp, :], op=ALU.add), reads=[psn, stv], writes=[stv])
                    gcol = E1[:, c * Lc + Lc - 1:c * Lc + Lc]
                    S.op("dve", lambda e, gcol=gcol: e.tensor_scalar(out=ST[:, hp, :], in0=ST[:, hp, :], scalar1=gcol, scalar2=None, op0=ALU.mult),
                         reads=[stv, E1], writes=[stv])
                    S.op("act", lambda e: e.activation(out=STB[:, hp, :], in_=ST[:, hp, :], func=AF.Copy), reads=[stv], writes=[stbv])
                if cfg.get("stop") == "seq" and l == cfg.get("stop_layer", 0):
                    raise StopBuild()
                S.op("act", lambda e: e.activation(out=Yf[:, 0:n], in_=ypb[:, 0:n], func=AF.Copy), reads=[ypb], writes=[Yf])
                S.op("dve", lambda e: e.tensor_copy(out=T3B[:, 0:n], in_=Yf[:, 0:n]), reads=[Yf], writes=[T3B])
                S.op("pe", lambda e: e.matmul(PB[0][:, 0:n], lhsT=BLB[:, :], rhs=T3B[:, 0:n], start=True, stop=True), reads=[BLB, T3B], writes=[PB[0]])
                S.op("dve", lambda e: e.scalar_tensor_tensor(out=Yf[:, 0:n], in0=PB[0][:, 0:n], scalar=-1.0 / 64, in1=Yf[:, 0:n], op0=ALU.mult, op1=ALU.add),
                     reads=[PB[0], Yf], writes=[Yf])
                S.op("act", lambda e: e.activation(out=T3B[:, 0:n], in_=Yf[:, 0:n], func=AF.Square), reads=[Yf], writes=[T3B])
                S.op("pe", lambda e: e.matmul(PB[1][:, 0:n], lhsT=BLB[:, :], rhs=T3B[:, 0:n], start=True, stop=True), reads=[BLB, T3B], writes=[PB[1]])
                S.op("act", lambda e: e.activation(out=T1[:, 0:n], in_=PB[1][:, 0:n], func=AF.Sqrt, bias=EPS_G, scale=1.0 / 64), reads=[PB[1], CON], writes=[T1])
                S.op("dve", lambda e: e.reciprocal(out=T1[:, 0:n], in_=T1[:, 0:n]), reads=[T1], writes=[T1])
                S.op("dve", lambda e: e.tensor_tensor(out=Yf[:, 0:n], in0=Yf[:, 0:n], in1=T1[:, 0:n], op=ALU.mult), reads=[Yf, T1], writes=[Yf])
                S.op("dve", lambda e, hp=hp: e.tensor_scalar(out=Yf[:, 0:n], in0=Yf[:, 0:n], scalar1=vc(4, hp), scalar2=vc(5, hp), op0=ALU.mult, op1=ALU.add),
                     reads=[Yf, VEC], writes=[Yf])
                S.op("dve", lambda e: e.tensor_tensor(out=T2[:, 0:n], in0=Cc[:, 0:n], in1=Vf[:, 0:n], op=ALU.mult), reads=[Cc, Vf], writes=[T2])
                S.op("dve", lambda e: e.tensor_tensor(out=Yf[:, 0:n], in0=Yf[:, 0:n], in1=T2[:, 0:n], op=ALU.add), reads=[Yf, T2], writes=[Yf])
                ygv = sub(YG, None, hp * NA * 2, (hp + 1) * NA * 2)
                S.op("dve", lambda e, hp=hp: e.tensor_tensor(out=YG[:, hp, 0:n], in0=Yf[:, 0:n], in1=Gf[:, 0:n], op=ALU.mult), reads=[Yf, Gf], writes=[ygv])
                if cfg.get("stop") == "hp0" and l == cfg.get("stop_layer", 0):
                    raise StopBuild()
            flush_wb(0)
            if last:
                S.dma("sp", wkvO[l, 1 if sample else 0], ST[:, :, :], reads=[ST], slot="wkvo")
            if cfg.get("stop") == "p2" and l == cfg.get("stop_layer", 0):
                raise StopBuild()

            sb.pos = region_pos
            HO = sb.alloc(F32, [128, KT, NA])
            ffn_pos = sb.pos
            WO = [WCH[0][0], WCH[1][0]]
            for oc in range(KT):
                wo = WO[oc % 2]
                wload(wo, wo_d, lambda t, oc=oc: t[l, oc], ("wch", oc % 2), (l, oc))
                pb = PB[oc % 2]
                proj_fm(pb, wo, YG, n)
                S.op("act", lambda e, oc=oc, pb=pb: e.activation(out=HO[:, oc, 0:n], in_=pb[:, 0:n], func=AF.Copy), reads=[pb],
                     writes=[sub(HO, None, oc * NA * 4, (oc + 1) * NA * 4)])
            flush_wb(0)
            load_x(XT, l, t0, n)
            post_and_ffn(l, XT, HO, n, t0, ffn_pos)

        def lam_setup(j):
            sb_mark = sb.pos
            sb.pos = base_pos
            LR = sb.alloc(F32, [128, 512])
            T = sb.alloc(F32, [128, 128])
            A2 = sb.alloc(F32, [128, 4])
            S.dma("sp", LR[:, :], lam_d[:, j, :], writes=[LR], slot="lam")
            for i in range(2):
                S.op("dve", lambda e, i=i: e.tensor_tensor(out=T[:, :], in0=LR[:, (2 * i) * 128:(2 * i + 1) * 128], in1=LR[:, (2 * i + 1) * 128:(2 * i + 2) * 128], op=ALU.mult),
                     reads=[LR], writes=[T])
                S.op("dve", lambda e, i=i: e.reduce_sum(out=A2[:, i:i + 1], in_=T[:, :], axis=AX.X), reads=[T], writes=[A2])
            S.op("act", lambda e: e.activation(out=A2[:, 0:2], in_=A2[:, 0:2], func=AF.Exp), reads=[A2], writes=[A2])
            S.op("dve", lambda e: e.tensor_tensor(out=A2[:, 2:3], in0=A2[:, 0:1], in1=A2[:, 1:2], op=ALU.subtract), reads=[A2], writes=[A2])
            S.op("dve", lambda e: e.tensor_scalar(out=LAMV[:, 4 * j:4 * j + 1], in0=A2[:, 2:3], scalar1=lambda_init(2 + j), scalar2=None, op0=ALU.add),
                 reads=[A2], writes=[LAMV])
            S.op("dve", lambda e: e.tensor_scalar(out=LAMV[:, 4 * j + 1:4 * j + 2], in0=LAMV[:, 4 * j:4 * j + 1], scalar1=-1.0, scalar2=None, op0=ALU.mult),
                 reads=[LAMV], writes=[LAMV])
            sb.pos = sb_mark

        def attn_tile(l, t0, n, sample):
            j = l - 2
            NA = NS if sample else TT
            cur["NA"] = NA
            sb.pos = base_pos
            XT = sb.alloc(F32, [128, KT, NA])
            HO = sb.alloc(F32, [128, KT, NA])
            ffn_pos = sb.pos
            XNB = sb.alloc(BF16, [128, KT, NA])
            QT = sb.alloc(BF16, [128, KT, NA])
            ATT = sb.alloc(BF16, [128, KT, NA])
            WQ = [sb.alloc(BF16, [128, KT, 128]) for _ in range(2)]
            QF = sb.alloc(F32, [128, NA])
            TB = sb.alloc(BF16, [128, NA])
            a_pos = sb.pos
            load_x(XT, l, t0, n)
            load_rope(t0, n)
            rms_stats(XT, n, PB[7], 1.0 / D, EPS_N)
            rms_apply(XNB, XT, VEC_NM + l * 2 + 0, n)
            for hm in range(KT):
                wq = WQ[hm % 2]
                wload(wq, aq_d, lambda t, hm=hm: t[j, hm], ("wq", hm % 2), (j, hm))
                pb = PB[hm % 2]
                proj_fm(pb, wq, XNB, n)
                rope_evac(pb, n, t0, QF, TB, PB[2 + hm % 2])
                S.op("act", lambda e, hm=hm: e.activation(out=QT[:, hm, 0:n], in_=QF[:, 0:n], func=AF.Copy), reads=[QF],
                     writes=[sub(QT, None, hm * NA * 2, (hm + 1) * NA * 2)])
            flush_wb(0)
            scale = 128 ** -0.5
            if not sample:
                prompt_attention(j, t0, n, QT, ATT, a_pos, scale)
            else:
                sample_attention(j, t0, n, QT, ATT, a_pos, scale)
            sb.pos = a_pos
            WO = [sb.alloc(BF16, [128, KT, 128]) for _ in range(2)]
            for oc in range(KT):
                wo = WO[oc % 2]
                wload(wo, ao_d, lambda t, oc=oc: t[j, oc], ("wao", oc % 2), (j, oc))
                pb = PB[oc % 2]
                proj_fm(pb, wo, ATT, n)
                S.op("act", lambda e, oc=oc, pb=pb: e.activation(out=HO[:, oc, 0:n], in_=pb[:, 0:n], func=AF.Copy), reads=[pb],
                     writes=[sub(HO, None, oc * NA * 4, (oc + 1) * NA * 4)])
            flush_wb(0)
            post_and_ffn(l, XT, HO, n, t0, ffn_pos)

        def post_and_ffn_attn(l, XT, HO, n, t0):
            post_and_ffn(l, XT, HO, n, t0, sb_after(HO))

        def sb_after(v):
            return v.hi

        def finish_head(j, ops_v, RS, qrows, ATT_write):
            pass

        def prompt_attention(j, t0, n, QT, ATT, a_pos, scale):
            sb.pos = a_pos
            KTH = [sb.alloc(BF16, [128, 2, NP]) for _ in range(2)]
            VH = [sb.alloc(BF16, [128, 16, 256]) for _ in range(2)]
            P = sb.alloc(BF16, [128, NP])
            PT = sb.alloc(BF16, [128, 16, 128])
            MX = sb.alloc(F32, [128, 8])
            O0 = sb.alloc(F32, [128, 256])
            DF = sb.alloc(F32, [128, 256])
            JK = sb.alloc(F32, [128, 256])
            ON = sb.alloc(BF16, [128, 256])
            kmax = t0 + n
            nkt_all = kmax // 128
            SC = psb(0, 4)
            for h in range(8):
                kth, vh = KTH[h % 2], VH[h % 2]
                for m in range(2):
                    S.dma("sp", kth[:, m, 0:kmax], KTs[2 * h + m, :, 0:kmax], reads=[dres("KTs")], writes=[kth], slot=("kth", h % 2))
                S.dma("sp", vh[:, 0:nkt_all, :], Vs[0:kmax, h * 256:(h + 1) * 256].rearrange("(k p) e -> p k e", p=128), reads=[dres("Vs")], writes=[vh],
                      slot=("vh", h % 2))
                for qi in range(n // 128):
                    kend = t0 + (qi + 1) * 128
                    nkt = kend // 128
                    OPS = PB[6]
                    for m in range(2):
                        hm = 2 * h + m
                        nkb = (kend + 511) // 512
                        S.group("pe", [lambda e, kb=kb, m=m, hm=hm: e.matmul(SC[:, kb * 512:min(kend, (kb + 1) * 512)], lhsT=QT[:, hm, qi * 128:(qi + 1) * 128],
                                                                          rhs=kth[:, m, kb * 512:min(kend, (kb + 1) * 512)], start=True, stop=True)
                                       for kb in range(nkb)], reads=[QT, kth], writes=[SC])
                        S.op("dve", lambda e: e.tensor_tensor(out=SC[:, kend - 128:kend], in0=SC[:, kend - 128:kend], in1=cc("cmask"), op=ALU.add),
                             reads=[SC, CON], writes=[SC])
                        S.op("dve", lambda e: e.reduce_max(out=MX[:, 0:1], in_=SC[:, 0:kend], axis=AX.X), reads=[SC], writes=[MX])
                        S.op("dve", lambda e: e.tensor_scalar(out=MX[:, 1:2], in0=MX[:, 0:1], scalar1=-scale, scalar2=None, op0=ALU.mult), reads=[MX], writes=[MX])
                        S.op("act", lambda e, m=m: e.activation(out=P[:, 0:kend], in_=SC[:, 0:kend], func=AF.Exp, bias=MX[:, 1:2], scale=scale,
                                                               accum_out=MX[:, 2 + m:3 + m]), reads=[SC, MX], writes=[P, MX])
                        for g0 in range(0, nkt, 8):
                            g1 = min(nkt, g0 + 8)
                            pbt = psb16(4 + (g0 // 8) % 2)
                            S.group("pe", [lambda e, kt=kt, pbt=pbt: e.transpose(pbt[:, (kt - g0) * 128:(kt - g0 + 1) * 128], P[:, kt * 128:(kt + 1) * 128], IDB[:, :])
                                           for kt in range(g0, g1)], reads=[P, IDB], writes=[pbt])
                            S.op("act", lambda e, pbt=pbt: e.activation(out=PT[:, g0:g1, :], in_=pbt[:, 0:(g1 - g0) * 128].rearrange("p (a b) -> p a b", b=128), func=AF.Copy),
                                 reads=[pbt], writes=[PT])
                        S.group("pe", [lambda e, kt=kt, m=m: e.matmul(OPS[:, m * 256:(m + 1) * 256], lhsT=PT[:, kt, :], rhs=vh[:, kt, :], start=(kt == 0), stop=(kt == nkt - 1))
                                       for kt in range(nkt)], reads=[PT, vh], writes=[OPS])
                    head_finish(j, OPS, MX, 128, O0, DF, JK, ON,
                                lambda ec, pbt: S.op("act", lambda e: e.activation(out=ATT[:, 2 * h + ec, qi * 128:(qi + 1) * 128], in_=pbt[:, ec * 128:(ec + 1) * 128], func=AF.Copy),
                                                     reads=[pbt], writes=[sub(ATT, None, (2 * h + ec) * TT * 2, (2 * h + ec + 1) * TT * 2)]))

        def head_finish(j, OPS, MX, rows, O0, DF, JK, ON, write_fn):
            R = slice(0, rows)
            S.op("dve", lambda e: e.reciprocal(out=MX[R, 4:6], in_=MX[R, 2:4]), reads=[MX], writes=[MX])
            S.op("dve", lambda e: e.tensor_tensor(out=MX[R, 6:7], in0=MX[R, 5:6], in1=LAMV[R, 4 * j + 1:4 * j + 2], op=ALU.mult), reads=[MX, LAMV], writes=[MX])
            S.op("dve", lambda e: e.tensor_scalar(out=O0[R, :], in0=OPS[R, 0:256], scalar1=MX[R, 4:5], scalar2=None, op0=ALU.mult), reads=[OPS, MX], writes=[O0])
            S.op("dve", lambda e: e.scalar_tensor_tensor(out=DF[R, :], in0=OPS[R, 256:512], scalar=MX[R, 6:7], in1=O0[R, :], op0=ALU.mult, op1=ALU.add),
                 reads=[OPS, MX, O0], writes=[DF])
            sublayer_norm(j, DF, rows, MX, JK, ON, write_fn)

        def sublayer_norm(j, DF, rows, MX, JK, ON, write_fn):
            R = slice(0, rows)
            S.op("act", lambda e: e.activation(out=JK[R, :], in_=DF[R, :], func=AF.Square, accum_out=MX[R, 7:8]), reads=[DF], writes=[JK, MX])
            S.op("act", lambda e: e.activation(out=MX[R, 7:8], in_=MX[R, 7:8], func=AF.Sqrt, bias=EPS_S[R, :], scale=1.0 / 256), reads=[MX, CON], writes=[MX])
            S.op("dve", lambda e: e.reciprocal(out=MX[R, 7:8], in_=MX[R, 7:8]), reads=[MX], writes=[MX])
            S.op("dve", lambda e: e.tensor_scalar(out=MX[R, 7:8], in0=MX[R, 7:8], scalar1=1.0 - lambda_init(2 + j), scalar2=None, op0=ALU.mult), reads=[MX], writes=[MX])
            S.op("dve", lambda e: e.scalar_tensor_tensor(out=ON[R, :], in0=DF[R, :], scalar=MX[R, 7:8], in1=SUBW[R, j, :], op0=ALU.mult, op1=ALU.mult),
                 reads=[DF, MX, SUBW], writes=[ON])
            pbt = psb16(7)
            S.group("pe", [lambda e, ec=ec: e.transpose(pbt[:, ec * 128:ec * 128 + rows], ON[R, ec * 128:(ec + 1) * 128], IDB[R, R]) for ec in range(2)],
                    reads=[ON, IDB], writes=[pbt])
            for ec in range(2):
                write_fn(ec, pbt)

        def sample_attention(j, t0, n, QT, ATT, a_pos, scale):
            sb.pos = a_pos
            QB = sb.alloc(BF16, [128, 16, 128])
            IDXF = sb.alloc(F32, [128, NPAGE])
            IDX = sb.alloc(I32, [128, NPAGE])
            PTI = sb.alloc(I32, [128, NPAGE])
            KP = [sb.alloc(BF16, [128, D]) for _ in range(2)]
            KTP = [sb.alloc(BF16, [128, 16, 128]) for _ in range(2)]
            NKEY = NPAGE * PAGE + NS
            SCS = sb.alloc(F32, [128, NKEY])
            PS_ = sb.alloc(BF16, [128, NKEY])
            PTB = [sb.alloc(BF16, [128, 128]) for _ in range(2)]
            KN = sb.alloc(BF16, [128, 16, NS])
            VN = sb.alloc(BF16, [NS, D])
            MX = sb.alloc(F32, [128, 8])
            OS = sb.alloc(F32, [128, 8, 256])
            OSEL = sb.alloc(F32, [128, 256])
            DM = sb.alloc(F32, [128, 64])
            O0 = sb.alloc(F32, [128, 256]); DF = sb.alloc(F32, [128, 256]); JK = sb.alloc(F32, [128, 256]); ON = sb.alloc(BF16, [128, 256])
            S.op("pool", lambda e: e.memset(QB[:, :, :], 0.0), writes=[QB])
            for hm in range(16):
                h, m = hm // 2, hm % 2
                c0 = m * 64 + h * 8
                S.op("dve", lambda e, hm=hm, c0=c0: e.tensor_copy(out=QB[:, hm, c0:c0 + NS], in_=QT[:, hm, 0:NS]), reads=[QT], writes=[QB])
            S.dma("sp", PTI[:, :], ptab.partition_broadcast(128), writes=[PTI], slot="pti")
            S.op("dve", lambda e: e.tensor_copy(out=IDXF[:, :], in_=PTI[:, :]), reads=[PTI], writes=[IDXF])
            S.op("dve", lambda e: e.tensor_scalar(out=IDXF[:, :], in0=IDXF[:, :], scalar1=128.0, scalar2=cc("pidx")[:, 0:1], op0=ALU.mult, op1=ALU.add),
                 reads=[IDXF, CON], writes=[IDXF])
            S.op("dve", lambda e: e.tensor_copy(out=IDX[:, :], in_=IDXF[:, :]), reads=[IDXF], writes=[IDX])

            def gather(dst, src_d, pg, slot):
                S.dma("pool", None, None, reads=[IDX], writes=[dst], slot=slot,
                      fn=lambda e: e.indirect_dma_start(out=dst[:, :], out_offset=None, in_=src_d,
                                                        in_offset=bass.IndirectOffsetOnAxis(ap=IDX[:, pg:pg + 1].bitcast(U32), axis=0)))

            for pg in range(NPAGE):
                kp, ktp = KP[pg % 2], KTP[pg % 2]
                gather(kp, cache_k, pg, ("kp", pg % 2))
                for half in range(2):
                    pbt = psb16(half)
                    S.group("pe", [lambda e, hm=hm, pbt=pbt: e.transpose(pbt[:, (hm % 8) * 128:(hm % 8 + 1) * 128], kp[:, hm * 128:(hm + 1) * 128], IDB[:, :])
                                   for hm in range(half * 8, half * 8 + 8)], reads=[kp, IDB], writes=[pbt])
                    S.op("act" if half == 0 else "dve",
                         (lambda e, pbt=pbt, half=half: e.activation(out=ktp[:, half * 8:half * 8 + 8, :], in_=pbt[:, :].rearrange("p (a b) -> p a b", b=128), func=AF.Copy)) if half == 0 else
                         (lambda e, pbt=pbt, half=half: e.tensor_copy(out=ktp[:, half * 8:half * 8 + 8, :], in_=pbt[:, :].rearrange("p (a b) -> p a b", b=128))),
                         reads=[pbt], writes=[ktp])
                pbs = PB[2 + (pg // 4) % 2]
                S.group("pe", [lambda e, hm=hm: e.matmul(pbs[:, (pg % 4) * 128:(pg % 4 + 1) * 128], lhsT=QB[:, hm, :], rhs=ktp[:, hm, :], start=(hm == 0), stop=(hm == 15))
                               for hm in range(16)], reads=[QB, ktp], writes=[pbs])
                if pg % 4 == 3:
                    g = pg // 4
                    S.op("act", lambda e, g=g, pbs=pbs: e.activation(out=SCS[:, g * 512:(g + 1) * 512], in_=pbs[:, :], func=AF.Copy), reads=[pbs],
                         writes=[sub(SCS, None, g * 2048, (g + 1) * 2048)])
            S.dma("sp", KN[:, :, :], KTs[:, :, NP:NP + NS].rearrange("k p t -> p k t"), reads=[dres("KTs")], writes=[KN], slot="kn")
            S.dma("sp", VN[:, :], Vs[NP:NP + NS, :], reads=[dres("Vs")], writes=[VN], slot="vn")
            pbs = PB[4]
            S.group("pe", [lambda e, hm=hm: e.matmul(pbs[:, 0:NS], lhsT=QB[:, hm, :], rhs=KN[:, hm, :], start=(hm == 0), stop=(hm == 15)) for hm in range(16)],
                    reads=[QB, KN], writes=[pbs])
            tail = sub(SCS, None, NPAGE * PAGE * 4, NKEY * 4)
            S.op("dve", lambda e: e.tensor_tensor(out=SCS[:, NPAGE * PAGE:NKEY], in0=pbs[:, 0:NS], in1=cc("smask"), op=ALU.add), reads=[pbs, CON], writes=[tail])
            S.op("dve", lambda e: e.reduce_max(out=MX[:, 0:1], in_=SCS[:, :], axis=AX.X), reads=[SCS], writes=[MX])
            S.op("dve", lambda e: e.tensor_scalar(out=MX[:, 1:2], in0=MX[:, 0:1], scalar1=-scale, scalar2=None, op0=ALU.mult), reads=[MX], writes=[MX])
            S.op("act", lambda e: e.activation(out=PS_[:, :], in_=SCS[:, :], func=AF.Exp, bias=MX[:, 1:2], scale=scale, accum_out=MX[:, 2:3]),
                 reads=[SCS, MX], writes=[PS_, MX])
            OA = psb(4, 4)
            nblk = NPAGE + 1
            for pg in range(nblk):
                lastb = pg == NPAGE
                kk_ = NS if lastb else 128
                ptb = PTB[pg % 2]
                pbt = psb16(pg % 2)
                S.op("pe", lambda e, pg=pg, kk_=kk_, pbt=pbt: e.transpose(pbt[0:kk_, 0:128], PS_[:, pg * 128:pg * 128 + kk_], IDB[:, :]), reads=[PS_, IDB], writes=[pbt])
                S.op("act", lambda e, kk_=kk_, pbt=pbt, ptb=ptb: e.activation(out=ptb[0:kk_, :], in_=pbt[0:kk_, 0:128], func=AF.Copy), reads=[pbt], writes=[ptb])
                if lastb:
                    vsrc = VN
                    vap = lambda f: VN[0:NS, f * 512:(f + 1) * 512]
                else:
                    vsrc = KP[pg % 2]
                    gather(vsrc, cache_v, pg, ("kp", pg % 2))
                    vap = lambda f, vsrc=vsrc: vsrc[:, f * 512:(f + 1) * 512]
                S.group("pe", [lambda e, f=f, vap=vap, ptb=ptb, kk_=kk_: e.matmul(OA[:, f * 512:(f + 1) * 512], lhsT=ptb[0:kk_, :], rhs=vap(f), start=(pg == 0), stop=lastb,
                                                                           skip_group_check=True)
                               for f in range(4)], reads=[ptb, vsrc], writes=[OA])
            S.op("dve", lambda e: e.tensor_tensor(out=OS[:, :, :], in0=OA[:, :].rearrange("p (h e) -> p h e", h=8),
                                                  in1=cc("selh").unsqueeze(2).to_broadcast([128, 8, 256]), op=ALU.mult), reads=[OA, CON], writes=[OS])
            S.op("dve", lambda e: e.tensor_reduce(out=OSEL[:, :], in_=OS[:, :, :].rearrange("p h e -> p e h"), op=ALU.add, axis=AX.X), reads=[OS], writes=[OSEL])
            S.op("dve", lambda e: e.reciprocal(out=MX[:, 3:4], in_=MX[:, 2:3]), reads=[MX], writes=[MX])
            S.op("dve", lambda e: e.tensor_scalar(out=OSEL[:, :], in0=OSEL[:, :], scalar1=MX[:, 3:4], scalar2=None, op0=ALU.mult), reads=[OSEL, MX], writes=[OSEL])
            S.op("dve", lambda e: e.tensor_copy(out=DM[0:64, :], in_=cc("dsel", 64)), reads=[CON], writes=[DM])
            S.op("dve", lambda e: e.tensor_scalar(out=DM[64:128, :], in0=cc("dsel")[64:128, :], scalar1=LAMV[64:128, 4 * j + 1:4 * j + 2], scalar2=None, op0=ALU.mult),
                 reads=[CON, LAMV], writes=[DM])
            pd = PB[0]
            S.op("pe", lambda e: e.matmul(pd[0:64, 0:256], lhsT=DM[:, :], rhs=OSEL[:, :], start=True, stop=True), reads=[DM, OSEL], writes=[pd])
            S.op("act", lambda e: e.activation(out=DF[0:64, :], in_=pd[0:64, 0:256], func=AF.Copy), reads=[pd], writes=[DF])

            def wr(ec, pbt):
                for h in range(8):
                    S.op("act", lambda e, h=h: e.activation(out=ATT[:, 2 * h + ec, 0:NS], in_=pbt[:, ec * 128 + h * 8:ec * 128 + h * 8 + 8], func=AF.Copy),
                         reads=[pbt], writes=[ATT])
            sublayer_norm(j, DF, 64, MX, JK, ON, wr)

        tiles = [(i * TT, TT, i == 0, i == NP // TT - 1, False) for i in range(NP // TT)] + [(NP, NS, True, True, True)]
        tiles = tiles[:cfg["tiles"]] if cfg["tiles"] < 5 else tiles
        if cfg.get("only_sample"):
            tiles = [tiles[-1]]
        try:
            for l in range(cfg["layers"]):
                if l >= 2:
                    lam_setup(l - 2)
                for (t0, n, first, last, sample) in tiles:
                    if l < 2:
                        rwkv_tile(l, t0, n, first, last, sample)
                    else:
                        attn_tile(l, t0, n, sample)
        except StopBuild:
            pass
        flush_wb(0)
        S.finish("sp")
        print("instructions emitted:", S.ninst, "dma sems:", len(S.dsem), "sbuf peak?", sb.pos, "cnt", S.cnt, "maxdma", max(v[1] for v in S.dsem.values()))
    return nc


def prepare_shared(inp):
    g = lambda k: np.asarray(inp[k])
    sh = {}
    sh["consts"] = make_consts()
    vecs = np.zeros((128, NV, KT), np.float32)
    for l in range(4):
        for i in range(2):
            vecs[:, VEC_NM + l * 2 + i] = vec_layout(g("norm_mix")[l, i])
            vecs[:, VEC_NF + l * 2 + i] = vec_layout(g("norm_ffn")[l, i])
    for l in range(2):
        for i in range(6):
            vecs[:, VEC_MU + l * 6 + i] = vec_layout(g("rwkv_mu")[l, i])
            vecs[:, VEC_VC + l * 6 + i] = vec_layout(g("rwkv_vec")[l, i])
        vecs[:, VEC_RK + l] = vec_layout(g("rwkv_rk")[l].reshape(-1))
    vecs[:, VEC_V0] = vec_layout(g("rwkv_v0")[0])
    vecs[:, VEC_KVN] = vec_layout(g("kv_norm"))
    sh["vecs"] = vecs
    cos, sin = rope_tables()
    sh["cos"], sh["sin"] = cos, sin
    sh["lamrep"] = np.ascontiguousarray(np.broadcast_to(g("attn_lambda").reshape(1, 2, 512), (128, 2, 512))).astype(np.float32)
    subw = np.stack([np.broadcast_to(g("attn_subln")[j][None, :], (128, 256)) for j in range(2)], axis=1).astype(np.float32)
    sh["subw"] = np.ascontiguousarray(subw)
    for nm, key in [("wr", "rwkv_wr"), ("wk", "rwkv_wk"), ("wv", "rwkv_wv"), ("wo", "rwkv_wo"), ("aq", "attn_wq"), ("ao", "attn_wo")]:
        sh[nm] = np.stack([w_cols(g(key)[l]) for l in range(2)])
    sh["lw1"] = np.stack([np.ascontiguousarray(g("rwkv_w1")[l].reshape(KT, 128, 96).transpose(1, 0, 2)) for l in range(2)])
    sh["la1"] = np.stack([np.ascontiguousarray(g("rwkv_a1")[l].reshape(KT, 128, 96).transpose(1, 0, 2)) for l in range(2)])
    sh["lv1"] = np.ascontiguousarray(g("rwkv_v1")[0].reshape(KT, 128, 64).transpose(1, 0, 2))[None]
    sh["lg1"] = np.stack([np.ascontiguousarray(g("rwkv_g1")[l].reshape(KT, 128, 256).transpose(1, 0, 2)) for l in range(2)])
    sh["lw2"] = np.ascontiguousarray(g("rwkv_w2"))
    sh["la2"] = np.ascontiguousarray(g("rwkv_a2"))
    sh["lv2"] = np.ascontiguousarray(g("rwkv_v2"))
    sh["lg2"] = np.stack([np.ascontiguousarray(g("rwkv_g2")[l].reshape(2, 128, D).transpose(1, 0, 2)) for l in range(2)])
    sh["kvk"] = w_cols(g("kv_wk"))
    sh["kvv"] = w_cols(g("kv_wv"), 512)
    sh["f1"] = np.stack([w_cols(g("ffn_w1")[l]) for l in range(4)])
    w2 = g("ffn_w2")
    sh["f2"] = np.stack([np.ascontiguousarray(w2[l].reshape(64, 128, KT, 128).transpose(2, 1, 0, 3)) for l in range(4)])
    sh["cache_k"] = g("cache_k").reshape(NPOOL * PAGE, D)
    sh["cache_v"] = g("cache_v").reshape(NPOOL * PAGE, D)
    return sh


def core_inputs(inp, sh, c):
    g = lambda k: np.asarray(inp[k])
    m = dict(sh)
    xs = np.concatenate([g("x_prompt")[c // 2], g("x_sample")[c]], axis=0)
    m["x0"] = np.ascontiguousarray(xs.T.reshape(KT, 128, NTOK))
    m["shift0"] = np.stack([vec_layout(g("state_shift")[l, c]) for l in range(2)])
    wk = g("state_wkv")[:, c]
    m["wkv0"] = np.ascontiguousarray(wk.reshape(2, KT, 2, 64, 64).transpose(0, 2, 4, 1, 3).reshape(2, 128, KT, 64))
    m["ptab"] = np.ascontiguousarray(g("page_table")[c].reshape(1, NPAGE).astype(np.int32))
    return m


def assemble(results):
    yp = np.zeros((4, NP, D), np.float32); ys = np.zeros((8, NS, D), np.float32)
    kp = np.zeros((4, NP, 8, 2, 128), np.float32); vp = np.zeros((4, NP, 8, 256), np.float32)
    ks = np.zeros((8, NS, 8, 2, 128), np.float32); vs = np.zeros((8, NS, 8, 256), np.float32)
    shp = np.zeros((2, 4, D), np.float32); shs = np.zeros((2, 8, D), np.float32)
    wp = np.zeros((2, 4, 32, 64, 64), np.float32); ws = np.zeros((2, 8, 32, 64, 64), np.float32)
    for c, r in enumerate(results):
        y = r["yT"].reshape(D, NTOK).T
        k = r["kT"].reshape(D, NTOK).T
        v = r["vO"]
        ys[c] = y[NP:]
        ks[c] = k[NP:].reshape(NS, 8, 2, 128)
        vs[c] = v[NP:].reshape(NS, 8, 256)
        sho = r["shO"]
        wo = r["wkvO"]
        wfix = lambda a: a.reshape(2, 64, KT, 64).transpose(2, 0, 3, 1).reshape(32, 64, 64)
        for l in range(2):
            shs[l, c] = sho[l, 1].T.reshape(D)
            ws[l, c] = wfix(wo[l, 1])
        if c % 2 == 0:
            s = c // 2
            yp[s] = y[:NP]
            kp[s] = k[:NP].reshape(NP, 8, 2, 128)
            vp[s] = v[:NP].reshape(NP, 8, 256)
            for l in range(2):
                shp[l, s] = sho[l, 0].T.reshape(D)
                wp[l, s] = wfix(wo[l, 0])
    return (yp, ys, kp, vp, shp, wp, ks, vs, shs, ws)


def kernel(**inputs):
    sh = prepare_shared(inputs)
    in_maps = [core_inputs(inputs, sh, c) for c in range(8)]
    nc = build_program(CFG)
    res = run_bass_kernel_spmd(nc, in_maps, core_ids=list(range(8)))
    return assemble(res.results)
```
